# Optimizing a Trainium2 kernel written in Bass

```python
import jax, jax.numpy as jnp
from jax import lax
import numpy as np

D_MODEL = 2048
BATCH = 8
SEQ = 4096
DEPTH = 4

HEAD_DIM = 128
BLOCK = 128
EPS = 1e-6
ROPE_THETA = 500000.0
PARTIAL_ROPE = HEAD_DIM // 4
FOX_HEADS = (D_MODEL // 4) // HEAD_DIM
FOX_W = FOX_HEADS * HEAD_DIM
FORGET_BIAS_CENTER = 2.0
MLA_HEADS = (D_MODEL // 2) // HEAD_DIM
MLA_Q_RANK = D_MODEL // 4
MLA_KV_RANK = D_MODEL // 4
MLA_NOPE = 128
MLA_ROPE = 64
MLA_V = HEAD_DIM
MLA_W = MLA_HEADS * MLA_V
DIL_HEADS = (D_MODEL // 4) // HEAD_DIM
DIL_W = DIL_HEADS * HEAD_DIM
DIL_BRANCHES = ((128, 1), (512, 4), (2048, 16))
MIX_W = FOX_W + MLA_W + DIL_W
IN_SIZES = (FOX_W, FOX_W, FOX_W, FOX_HEADS,
            MLA_Q_RANK, MLA_KV_RANK, MLA_ROPE,
            DIL_W, DIL_W, DIL_W)
IN_W = sum(IN_SIZES)
D_FF = 5632

kernel_name = "hybrid_fox_mla_dilated_macaron"


def rms_norm(x, g):
    xf = x.astype(jnp.float32)
    y = xf * lax.rsqrt(jnp.mean(xf * xf, axis=-1, keepdims=True) + EPS)
    return (y * g.astype(jnp.float32)).astype(x.dtype)


def swiglu(h, w_gate, w_up, w_down):
    return (jax.nn.silu(h @ w_gate) * (h @ w_up)) @ w_down


def rope_tables(seq, dim):
    inv = 1.0 / (ROPE_THETA ** (jnp.arange(0, dim, 2, dtype=jnp.float32) / dim))
    ang = jnp.arange(seq, dtype=jnp.float32)[:, None] * inv[None, :]
    return jnp.cos(ang), jnp.sin(ang)


def apply_rope(x, cos, sin):
    x1, x2 = jnp.split(x, 2, axis=-1)
    c, s = cos.astype(x.dtype), sin.astype(x.dtype)
    return jnp.concatenate([x1 * c - x2 * s, x1 * s + x2 * c], axis=-1)


def partial_rope(x, cos, sin):
    return jnp.concatenate([apply_rope(x[..., :PARTIAL_ROPE], cos, sin), x[..., PARTIAL_ROPE:]], axis=-1)


def to_heads(t, n_heads):
    b, s, _ = t.shape
    return t.reshape(b, s, n_heads, -1).transpose(0, 2, 1, 3)


def merge_heads(t):
    b, h, s, d = t.shape
    return t.transpose(0, 2, 1, 3).reshape(b, s, h * d)


def causal_block_attention(q, k, v, scale, cum_log_f=None):
    b, h, s_len, _ = q.shape
    nb = s_len // BLOCK
    kpos = jnp.arange(s_len)
    xs = [jnp.arange(nb), q.reshape(b, h, nb, BLOCK, -1).transpose(2, 0, 1, 3, 4)]
    if cum_log_f is not None:
        xs.append(cum_log_f.reshape(b, h, nb, BLOCK).transpose(2, 0, 1, 3))

    def attend(blk):
        i, qi = blk[0], blk[1]
        sc = jnp.einsum("bhqd,bhkd->bhqk", qi, k, preferred_element_type=jnp.float32) * scale
        if cum_log_f is not None:
            sc = sc + (blk[2][..., :, None] - cum_log_f[..., None, :])
        qpos = i * BLOCK + jnp.arange(BLOCK)
        sc = jnp.where(kpos[None, :] <= qpos[:, None], sc, -jnp.inf)
        p = jax.nn.softmax(sc, axis=-1)
        return jnp.einsum("bhqk,bhkd->bhqd", p.astype(v.dtype), v)

    out = lax.map(attend, tuple(xs))
    return out.transpose(1, 2, 0, 3, 4).reshape(b, h, s_len, -1)


def dilated_branch(q, k, v, window, dilation):
    b, h, s_len, hd = q.shape
    L = s_len // dilation
    n_back = window // dilation
    Lp = -(-L // BLOCK) * BLOCK
    nb = Lp // BLOCK

    def to_blocks(t):
        t = t.reshape(b, h, L, dilation, hd).transpose(0, 1, 3, 2, 4)
        t = jnp.pad(t, ((0, 0), (0, 0), (0, 0), (0, Lp - L), (0, 0)))
        return t.reshape(b, h, dilation, nb, BLOCK, hd)

    def with_prev(t):
        prev = jnp.pad(t, ((0, 0), (0, 0), (0, 0), (1, 0), (0, 0), (0, 0)))[:, :, :, :-1]
        return jnp.concatenate([prev, t], axis=4)

    qb = to_blocks(q)
    kc = with_prev(to_blocks(k))
    vc = with_prev(to_blocks(v))
    sc = jnp.einsum("bhrnqd,bhrnkd->bhrnqk", qb, kc, preferred_element_type=jnp.float32) * (hd ** -0.5)
    kidx = jnp.arange(2 * BLOCK)
    dist = (BLOCK + jnp.arange(BLOCK))[:, None] - kidx[None, :]
    band = (dist >= 0) & (dist <= n_back)
    has_prev = (jnp.arange(nb)[:, None, None] > 0) | (kidx[None, None, :] >= BLOCK)
    sc = jnp.where(band[None] & has_prev, sc, -jnp.inf)
    m = jnp.max(sc, axis=-1, keepdims=True)
    e = jnp.exp(sc - m)
    l = jnp.sum(e, axis=-1, keepdims=True)
    o = jnp.einsum("bhrnqk,bhrnkd->bhrnqd", (e / l).astype(v.dtype), vc)
    lse = (m + jnp.log(l))[..., 0]
    o = o.reshape(b, h, dilation, Lp, hd)[:, :, :, :L].transpose(0, 1, 3, 2, 4).reshape(b, h, s_len, hd)
    lse = lse.reshape(b, h, dilation, Lp)[..., :L].transpose(0, 1, 3, 2).reshape(b, h, s_len)
    return o, lse


def dilated_mixture(q, k, v):
    outs, lses = [], []
    for window, dilation in DIL_BRANCHES:
        o, lse = dilated_branch(q, k, v, window, dilation)
        outs.append(o)
        lses.append(lse)
    wts = jax.nn.softmax(jnp.stack(lses, axis=0), axis=0)
    return jnp.sum(wts[..., None].astype(q.dtype) * jnp.stack(outs, axis=0), axis=0)


def setup_inputs(seed: int = 0) -> dict:
    key = jax.random.key(seed)
    ks = iter(jax.random.split(key, 32))

    def dense(shape, fan_in):
        return jax.random.normal(next(ks), shape, jnp.float32) * (fan_in ** -0.5)

    def gain(shape):
        return 1.0 + 0.02 * jax.random.normal(next(ks), shape, jnp.float32)

    x = jax.random.normal(next(ks), (BATCH, SEQ, D_MODEL), jnp.float32)
    return {
        "x": x,
        "ffn1_norm": gain((DEPTH, D_MODEL)),
        "ffn1_w_gate": dense((DEPTH, D_MODEL, D_FF), D_MODEL),
        "ffn1_w_up": dense((DEPTH, D_MODEL, D_FF), D_MODEL),
        "ffn1_w_down": dense((DEPTH, D_FF, D_MODEL), D_FF),
        "mix_norm": gain((DEPTH, D_MODEL)),
        "w_in": dense((DEPTH, D_MODEL, IN_W), D_MODEL),
        "fox_forget_bias": FORGET_BIAS_CENTER + 0.5 * jax.random.normal(next(ks), (DEPTH, FOX_HEADS), jnp.float32),
        "mla_q_norm": gain((DEPTH, MLA_Q_RANK)),
        "mla_kv_norm": gain((DEPTH, MLA_KV_RANK)),
        "mla_w_uq": dense((DEPTH, MLA_Q_RANK, MLA_HEADS * (MLA_NOPE + MLA_ROPE)), MLA_Q_RANK),
        "mla_w_ukv": dense((DEPTH, MLA_KV_RANK, MLA_HEADS * (MLA_NOPE + MLA_V)), MLA_KV_RANK),
        "w_out": dense((DEPTH, MIX_W, D_MODEL), MIX_W),
        "ffn2_norm": gain((DEPTH, D_MODEL)),
        "ffn2_w_gate": dense((DEPTH, D_MODEL, D_FF), D_MODEL),
        "ffn2_w_up": dense((DEPTH, D_MODEL, D_FF), D_MODEL),
        "ffn2_w_down": dense((DEPTH, D_FF, D_MODEL), D_FF),
        "final_norm": gain((D_MODEL,)),
    }


def reference(x, ffn1_norm, ffn1_w_gate, ffn1_w_up, ffn1_w_down, mix_norm, w_in, fox_forget_bias,
              mla_q_norm, mla_kv_norm, mla_w_uq, mla_w_ukv, w_out, ffn2_norm, ffn2_w_gate, ffn2_w_up,
              ffn2_w_down, final_norm):
    b, s_len, _ = x.shape
    cos_p, sin_p = rope_tables(s_len, PARTIAL_ROPE)
    cos_m, sin_m = rope_tables(s_len, MLA_ROPE)
    offsets = []
    acc = 0
    for size in IN_SIZES[:-1]:
        acc += size
        offsets.append(acc)

    for l in range(DEPTH):
        x = x + 0.5 * swiglu(rms_norm(x, ffn1_norm[l]), ffn1_w_gate[l], ffn1_w_up[l], ffn1_w_down[l])

        h = rms_norm(x, mix_norm[l])
        proj = h @ w_in[l]
        fq, fk, fv, f_logit, c_q, c_kv, k_r, dq, dk, dv = jnp.split(proj, offsets, axis=-1)

        log_f = jax.nn.log_sigmoid((f_logit + fox_forget_bias[l]).astype(jnp.float32))
        cum = jnp.cumsum(log_f, axis=1).transpose(0, 2, 1)
        out_a = causal_block_attention(to_heads(fq, FOX_HEADS), to_heads(fk, FOX_HEADS),
                                       to_heads(fv, FOX_HEADS), HEAD_DIM ** -0.5, cum)

        q_b = to_heads(rms_norm(c_q, mla_q_norm[l]) @ mla_w_uq[l], MLA_HEADS)
        q_b = jnp.concatenate([q_b[..., :MLA_NOPE], apply_rope(q_b[..., MLA_NOPE:], cos_m, sin_m)], axis=-1)
        kv_b = to_heads(rms_norm(c_kv, mla_kv_norm[l]) @ mla_w_ukv[l], MLA_HEADS)
        k_rope = apply_rope(k_r[:, None], cos_m, sin_m)
        k_b = jnp.concatenate([kv_b[..., :MLA_NOPE],
                               jnp.broadcast_to(k_rope, (b, MLA_HEADS, s_len, MLA_ROPE))], axis=-1)
        out_b = causal_block_attention(q_b, k_b, kv_b[..., MLA_NOPE:], (MLA_NOPE + MLA_ROPE) ** -0.5)

        out_c = dilated_mixture(partial_rope(to_heads(dq, DIL_HEADS), cos_p, sin_p),
                                partial_rope(to_heads(dk, DIL_HEADS), cos_p, sin_p),
                                to_heads(dv, DIL_HEADS))

        mixed = jnp.concatenate([merge_heads(out_a), merge_heads(out_b), merge_heads(out_c)], axis=-1)
        x = x + mixed @ w_out[l]

        x = x + 0.5 * swiglu(rms_norm(x, ffn2_norm[l]), ffn2_w_gate[l], ffn2_w_up[l], ffn2_w_down[l])

    return rms_norm(x, final_norm)
```

```python
import numpy as np
from contextlib import ExitStack
import concourse.bass as bass
import concourse.mybir as mybir
from concourse.bass_utils import run_bass_kernel_spmd

F32 = mybir.dt.float32
BF16 = mybir.dt.bfloat16
AF = mybir.ActivationFunctionType
ALU = mybir.AluOpType
ENGS = ('pe', 'act', 'dve', 'pool', 'sp')

D = 2048
DFF = 5632
KC = 16
FC = 44
T = 512
EPS = 1e-6
THETA = 500000.0
NFM = 34
NINX = NFM * 128 + 1024 + 4
NFMS = 37
PERSIST = ('wg_s', 'wu_s', 'wd_s', 'win_s', 'wfl_s', 'wuq_s', 'wukv_s', 'wout_s')


class Prog:
    def __init__(self, nc, ndsem=12):
        self.nc = nc
        self.es = ExitStack()
        self.q = {e: [] for e in ENGS}
        self.semh = {}
        self.ecnt = {e: 0 for e in ENGS}
        self.seen = {e: {} for e in ENGS}
        self.lastw = {}
        self.rds = {}
        self.dcnt = {}
        self.drr = {e: 0 for e in ENGS}
        self.ndsem = ndsem
        self.pe_pending = False
        for e in ENGS:
            self.semh[('e', e)] = self.es.enter_context(nc.semaphore("s_" + e))
        for e in ('sp', 'pool'):
            for i in range(ndsem):
                k = ('d', e, i)
                self.semh[k] = self.es.enter_context(nc.semaphore("d_%s_%d" % (e, i)))
                self.dcnt[k] = 0

    def _deps(self, eng, reads, writes, is_dma):
        own = ('e', eng)
        cand = {}

        def add(t, raw):
            if t is None:
                return
            k, v = t
            if not is_dma and k == own:
                if eng == 'pe' or not raw:
                    return
            if cand.get(k, 0) < v:
                cand[k] = v
        for r in reads:
            add(self.lastw.get(r), True)
        for w in writes:
            add(self.lastw.get(w), False)
            for t in self.rds.get(w, ()):
                add(t, False)
        out = []
        seen = self.seen[eng]
        for k, v in cand.items():
            if seen.get(k, 0) >= v:
                continue
            seen[k] = v
            out.append((k, v))
        return out

    def _record(self, t, reads, writes):
        for r in reads:
            lst = self.rds.setdefault(r, [])
            if not lst or lst[-1] != t:
                lst.append(t)
        for w in writes:
            self.lastw[w] = t
            self.rds[w] = []

    def op(self, eng, fn, reads=(), writes=(), tick=True):
        waits = self._deps(eng, reads, writes, False)
        if tick:
            self.ecnt[eng] += 1
            t = (('e', eng), self.ecnt[eng])
            if eng == 'pe':
                self.pe_pending = False
        else:
            assert eng == 'pe'
            t = (('e', eng), self.ecnt[eng] + 1)
            self.pe_pending = True
        self._record(t, reads, writes)
        self.q[eng].append((waits, fn, ('e', eng) if tick else None, 1))

    def dma(self, qeng, out, in_, reads=(), writes=()):
        i = self.drr[qeng]
        self.drr[qeng] = (i + 1) % self.ndsem
        k = ('d', qeng, i)
        waits = self._deps(qeng, reads, writes, True)
        prev = self.dcnt[k]
        if prev > 0 and self.seen[qeng].get(k, 0) < prev:
            self.seen[qeng][k] = prev
            waits.append((k, prev))
        self.dcnt[k] = prev + 16
        t = (k, prev + 16)
        self._record(t, reads, writes)
        self.q[qeng].append((waits, (lambda e: e.dma_start(out=out, in_=in_)), k, 16))

    def gate(self, qeng, eng):
        v = self.ecnt[eng]
        k = ('e', eng)
        if v > 0 and self.seen[qeng].get(k, 0) < v:
            self.seen[qeng][k] = v
            self.q[qeng].append(([(k, v)], None, None, 0))

    def barrier(self):
        assert not self.pe_pending
        tickets = [(k, v) for k, v in self.dcnt.items() if v > 0 and k[1] != 'pool']
        tickets += [(('e', e), self.ecnt[e]) for e in ENGS if self.ecnt[e] > 0]
        for e in ENGS:
            seen = self.seen[e]
            w = []
            for k, v in tickets:
                if k == ('e', e):
                    continue
                if seen.get(k, 0) < v:
                    seen[k] = v
                    w.append((k, v))
            if w:
                self.q[e].append((w, None, None, 0))
        keep = {k: v for k, v in self.lastw.items() if isinstance(k, tuple) and k[0] in PERSIST}
        self.lastw.clear()
        self.lastw.update(keep)
        self.rds.clear()

    def replay(self):
        nc = self.nc
        semh = self.semh
        q = self.q

        def run(e, items):
            for waits, fn, inck, incv in items:
                for k, v in waits:
                    e.wait_ge(semh[k], v)
                if fn is None:
                    continue
                ins = fn(e)
                if inck is not None:
                    ins.then_inc(semh[inck], incv)

        with nc.Block() as block:
            @block.tensor
            def _(e):
                run(e, q['pe'])

            @block.scalar
            def _(e):
                run(e, q['act'])

            @block.vector
            def _(e):
                run(e, q['dve'])

            @block.gpsimd
            def _(e):
                run(e, q['pool'])

            @block.sync
            def _(e):
                run(e, q['sp'])

    def close(self):
        self.es.close()


def smalls_layout(L):
    o = {}
    o['F1'] = 0
    o['MX'] = L * 16
    o['F2'] = 2 * L * 16
    o['FIN'] = 3 * L * 16
    o['QN'] = o['FIN'] + 16
    o['KVN'] = o['QN'] + 4 * L
    o['FB'] = o['KVN'] + 4 * L
    o['N'] = o['FB'] + L
    return o


def build(S, L, debug=False):
    NT = S // T
    NB = S // 128
    nc = bass.Bass("TRN2", target_bir_lowering=False)
    P = Prog(nc)
    SM = smalls_layout(L)

    def din(name, shape, dt=F32):
        return nc.dram_tensor(name, list(shape), dt, kind="ExternalInput").ap()

    def dscr(name, shape, dt):
        kind = "ExternalOutput" if (debug and not name.startswith('w')) else "Internal"
        return nc.dram_tensor(name, list(shape), dt, kind=kind).ap()

    x = din("x", [S, D])
    wg_in = [din("ffn1_w_gate", [L, D, DFF]), din("ffn2_w_gate", [L, D, DFF])]
    wu_in = [din("ffn1_w_up", [L, D, DFF]), din("ffn2_w_up", [L, D, DFF])]
    wd_in = [din("ffn1_w_down", [L, DFF, D]), din("ffn2_w_down", [L, DFF, D])]
    win_in = din("w_in_ext", [L, D, NINX])
    wuq_in = din("w_uq_ext", [L, 512, 2048])
    wukv_in = din("w_ukv_ext", [L, 512, 2048])
    wout_in = din("w_out", [L, D, D])
    smalls_in = din("smalls", [128, SM['N']])
    consts_in = din("consts", [128, 384])
    ropeC_in = din("ropeC", [2, 128, S])
    ropeM_in = din("ropeM", [2, 128, S])
    out = nc.dram_tensor("out", [S, D], F32, kind="ExternalOutput").ap()

    wg_s = [[dscr("wg_s%d_%d" % (l, w), [22, 128, 16, 256], BF16) for w in range(2)] for l in range(L)]
    wu_s = [[dscr("wu_s%d_%d" % (l, w), [22, 128, 16, 256], BF16) for w in range(2)] for l in range(L)]
    wd_s = [[dscr("wd_s%d_%d" % (l, w), [8, 128, 44, 256], BF16) for w in range(2)] for l in range(L)]
    win_s = [dscr("win_s%d" % l, [21, 128, 16, 256], BF16) for l in range(L)]
    wfl_s = [dscr("wfl_s%d" % l, [128, 16, 4], BF16) for l in range(L)]
    wuq_s = [dscr("wuq_s%d" % l, [128, 4, 2048], BF16) for l in range(L)]
    wukv_s = [dscr("wukv_s%d" % l, [128, 4, 2048], BF16) for l in range(L)]
    wout_s = [dscr("wout_s%d" % l, [8, 128, 16, 256], BF16) for l in range(L)]
    xT = dscr("xT", [D, S], F32)
    fmS = dscr("fmS", [NFMS, 128, S], BF16)
    vS = dscr("vS", [S, 2048], BF16)
    flogS = dscr("flogS", [4, S], F32)
    cpS = dscr("cpS", [2, 3, 4, S], BF16)
    mixT = dscr("mixT", [16, 128, S], BF16)

    es = P.es

    uid = [0]

    def sb(name, shape, dt, stack=None):
        uid[0] += 1
        return (stack or es).enter_context(nc.sbuf_tensor("%s_sb%d" % (name, uid[0]), list(shape), dt))

    ps = [es.enter_context(nc.psum_tensor("ps%d" % i, [128, 512], F32)) for i in range(8)]

    cst = sb("cst", [128, 384], F32)
    ident = cst[:, 0:128]
    cbf = sb("cbf", [128, 384], BF16)
    mask2 = cbf[:, 128:384]
    mask4 = sb("mask4", [128, 512], BF16)
    tri_ge = cbf[:, 256:384]
    ones_f = sb("ones_f", [128, 128], F32)
    ones_b = sb("ones_b", [128, 128], BF16)
    epsT = sb("epsT", [128, 1], F32)
    one1 = sb("one1", [128, 1], F32)
    smalls = sb("smalls", [128, SM['N']], F32)
    negb = sb("negb", [4, L], F32)

    P.dma('sp', cst[:], consts_in, writes=['cst'])
    P.dma('sp', smalls[:], smalls_in, writes=['smalls'])
    P.op('dve', lambda e: e.tensor_copy(out=cbf[:], in_=cst[:]), reads=['cst'], writes=['cbf'])
    P.op('dve', lambda e: e.tensor_copy(out=mask4[:, 0:256], in_=cst[:, 128:384]), reads=['cst'], writes=['mask4'])
    P.op('dve', lambda e: e.tensor_copy(out=mask4[:, 256:512], in_=cst[:, 128:384]), reads=['cst'], writes=['mask4'])
    P.op('dve', lambda e: e.memset(ones_f[:], 1.0), writes=['ones_f'])
    P.op('dve', lambda e: e.memset(ones_b[:], 1.0), writes=['ones_b'])
    P.op('dve', lambda e: e.memset(epsT[:], EPS), writes=['epsT'])
    P.op('dve', lambda e: e.memset(one1[:], 1.0), writes=['one1'])
    P.op('dve', lambda e: e.tensor_scalar_mul(out=negb[:], in0=smalls[0:4, SM['FB']:SM['FB'] + L], scalar1=-1.0),
         reads=['smalls'], writes=['negb'])

    def conv_ffn(l, w):
        for g in range(22):
            P.dma('pool', wg_s[l][w][g], wg_in[w][l, :, g * 256:(g + 1) * 256].rearrange("(kc p) c -> p kc c", p=128),
                  writes=[('wg_s', l, w, g)])
            P.dma('pool', wu_s[l][w][g], wu_in[w][l, :, g * 256:(g + 1) * 256].rearrange("(kc p) c -> p kc c", p=128),
                  writes=[('wu_s', l, w, g)])
        for g in range(8):
            for hh in range(2):
                P.dma('pool', wd_s[l][w][g][:, hh * 22:(hh + 1) * 22, :],
                      wd_in[w][l, hh * 2816:(hh + 1) * 2816, g * 256:(g + 1) * 256].rearrange("(kc p) c -> p kc c", p=128),
                      writes=[('wd_s', l, w, g, hh)])

    def conv_mix(l):
        for g in range(21):
            P.dma('pool', win_s[l][g], win_in[l, :, g * 256:(g + 1) * 256].rearrange("(kc p) c -> p kc c", p=128),
                  writes=[('win_s', l, g)])
        P.dma('pool', wfl_s[l], win_in[l, :, 5376:5380].rearrange("(kc p) c -> p kc c", p=128), writes=[('wfl_s', l)])
        P.dma('pool', wuq_s[l], wuq_in[l].rearrange("(kc p) c -> p kc c", p=128), writes=[('wuq_s', l)])
        P.dma('pool', wukv_s[l], wukv_in[l].rearrange("(kc p) c -> p kc c", p=128), writes=[('wukv_s', l)])
        for g in range(8):
            P.dma('pool', wout_s[l][g], wout_in[l, :, g * 256:(g + 1) * 256].rearrange("(kc p) c -> p kc c", p=128),
                  writes=[('wout_s', l, g)])

    conv_ffn(0, 0)
    conv_mix(0)

    def mm(out_ap, lhsT, rhs, start, stop, reads, writes, tick):
        P.op('pe', lambda e: e.matmul(out_ap, lhsT=lhsT, rhs=rhs, start=start, stop=stop), reads, writes, tick)

    cpy_rr = [0]

    def copy_any(out_ap, in_ap, reads, writes, eng=None):
        if eng is None:
            eng = 'act' if (cpy_rr[0] % 2 == 0) else 'dve'
            cpy_rr[0] += 1
        if eng == 'act':
            P.op('act', lambda e: e.copy(out=out_ap, in_=in_ap), reads, writes)
        else:
            P.op('dve', lambda e: e.tensor_copy(out=out_ap, in_=in_ap), reads, writes)

    def rms_p1(src_fn, nch, rkey_fn, sq, acc, akey):
        for c in range(nch):
            s = c % 2
            src = src_fn(c)
            if c == 0:
                P.op('act', (lambda e, src=src: e.activation(out=acc[:], in_=src, func=AF.Square)),
                     reads=[rkey_fn(c)], writes=[akey])
            else:
                P.op('act', (lambda e, s=s, src=src: e.activation(out=sq[:, s, :], in_=src, func=AF.Square)),
                     reads=[rkey_fn(c)], writes=[('sq', s)])
                P.op('dve', (lambda e, s=s: e.tensor_tensor(out=acc[:], in0=acc[:], in1=sq[:, s, :], op=ALU.add)),
                     reads=[('sq', s), akey], writes=[akey])

    def rms_p2(src_fn, nch, gcol_fn, dst_fn, dim, rkey_fn, wkey_fn, rstd, bank, rkey, acc, akey):
        mm(ps[bank][:], ones_f[:], acc[:], True, True, reads=[akey], writes=[('ps', bank)], tick=True)
        P.op('act', lambda e: e.activation(out=rstd[:], in_=ps[bank][:], func=AF.Sqrt, bias=epsT[:], scale=1.0 / dim),
             reads=[('ps', bank)], writes=[rkey])
        P.op('dve', lambda e: e.reciprocal(out=rstd[:], in_=rstd[:]), reads=[rkey], writes=[rkey])
        for c in range(nch):
            src = src_fn(c)
            dst = dst_fn(c)
            g = gcol_fn(c)
            P.op('dve', (lambda e, src=src, dst=dst, g=g: e.scalar_tensor_tensor(
                out=dst, in0=src, scalar=g, in1=rstd[:], op0=ALU.mult, op1=ALU.mult)),
                reads=[rkey_fn(c), rkey], writes=[wkey_fn(c)])

    def rmsnorm(src_fn, nch, gcol_fn, dst_fn, dim, rkey_fn, wkey_fn, sq, rstd, bank, rkey, acc=None):
        rms_p1(src_fn, nch, rkey_fn, sq, acc, 'acc')
        rms_p2(src_fn, nch, gcol_fn, dst_fn, dim, rkey_fn, wkey_fn, rstd, bank, rkey, acc, 'acc')

    def load_xt(xt, tt):
        P.dma('sp', xt[:], xT[:, tt * T:(tt + 1) * T].rearrange("(c p) t -> p c t", p=128),
              reads=[('xT', tt)], writes=[('xt', c) for c in range(KC)])

    def store_xt(xt, tt):
        P.dma('sp', xT[:, tt * T:(tt + 1) * T].rearrange("(c p) t -> p c t", p=128), xt[:],
              reads=[('xt', c) for c in range(KC)], writes=[('xT', tt)])

    def phase_T0():
        with ExitStack() as ph:
            xin = sb("xin", [128, 4, D], F32, ph)
            stg = sb("stg0", [128, KC, T], F32, ph)
            for tt in range(NT):
                P.dma('sp', xin[:], x[tt * T:(tt + 1) * T, :].rearrange("(b p) d -> p b d", p=128), writes=['xin'])
                for c in range(KC):
                    bank = c % 4
                    for b in range(4):
                        o_ap = ps[bank][:, b * 128:(b + 1) * 128]
                        i_ap = xin[:, b, c * 128:(c + 1) * 128]
                        P.op('pe', (lambda e, o_ap=o_ap, i_ap=i_ap: e.transpose(out=o_ap, in_=i_ap, identity=ident)),
                             reads=['xin', 'cst'], writes=[('ps', bank)], tick=(b == 3))
                    copy_any(stg[:, c, :], ps[bank][:], [('ps', bank)], [('stg', c)])
                P.dma('sp', xT[:, tt * T:(tt + 1) * T].rearrange("(c p) t -> p c t", p=128), stg[:],
                      reads=[('stg', c) for c in range(KC)], writes=[('xT', tt)])
            P.barrier()

    def phase_ffn(l, w):
        gbase = (SM['F1'] if w == 0 else SM['F2']) + l * 16
        with ExitStack() as ph:
            xt = sb("xt", [128, KC, T], F32, ph)
            hT = sb("hT", [128, 2, KC, T], BF16, ph)
            actT = sb("actT", [128, FC, T], BF16, ph)
            wgb = sb("wgb", [128, 2, 16, 256], BF16, ph)
            wub = sb("wub", [128, 2, 16, 256], BF16, ph)
            wdb = sb("wdb", [128, 2, 44, 256], BF16, ph)
            xr = sb("xr", [128, 2, T], F32, ph)
            acc = sb("acc", [128, T], F32, ph)
            sg = sb("sg", [128, 2, T], F32, ph)
            rstd = sb("rstd", [128, 2, T], F32, ph)
            ngu = [0]
            nd = [0]

            def ensure_gu(q):
                q = min(q, NT * 22 - 1)
                while ngu[0] <= q:
                    g_ = ngu[0] % 22
                    s_ = ngu[0] % 2
                    P.dma('sp', wgb[:, s_], wg_s[l][w][g_], reads=[('wg_s', l, w, g_)], writes=[('wgb', s_)])
                    P.dma('sp', wub[:, s_], wu_s[l][w][g_], reads=[('wu_s', l, w, g_)], writes=[('wub', s_)])
                    ngu[0] += 1

            def ensure_d(q):
                q = min(q, NT * 8 - 1)
                while nd[0] <= q:
                    g_ = nd[0] % 8
                    s_ = nd[0] % 2
                    for hh in range(2):
                        P.dma('sp', wdb[:, s_, hh * 22:(hh + 1) * 22, :], wd_s[l][w][g_][:, hh * 22:(hh + 1) * 22, :],
                              reads=[('wd_s', l, w, g_, hh)], writes=[('wdb', s_, hh)])
                    nd[0] += 1

            def ssq_load(tt, c):
                s_ = c % 2
                P.dma('sp', xr[:, s_, :], xT[c * 128:(c + 1) * 128, tt * T:(tt + 1) * T], writes=[('xr', s_)])

            def ssq_chunk(tt, c):
                s_ = c % 2
                if c == 0:
                    P.op('act', (lambda e: e.activation(out=acc[:], in_=xr[:, s_, :], func=AF.Square)),
                         reads=[('xr', s_)], writes=['acc'])
                else:
                    P.op('act', (lambda e: e.activation(out=xr[:, s_, :], in_=xr[:, s_, :], func=AF.Square)),
                         reads=[('xr', s_)], writes=[('xr', s_)])
                    P.op('dve', (lambda e: e.tensor_tensor(out=acc[:], in0=acc[:], in1=xr[:, s_, :], op=ALU.add)),
                         reads=[('xr', s_), 'acc'], writes=['acc'])

            def ssq_stream(tt):
                ssq_load(tt, 0)
                for c in range(KC):
                    if c + 1 < KC:
                        ssq_load(tt, c + 1)
                    ssq_chunk(tt, c)

            def ssq_finish(par):
                mm(ps[6][:], ones_f[:], acc[:], True, True, reads=['acc'], writes=[('ps', 6)], tick=True)
                P.op('act', lambda e: e.activation(out=rstd[:, par, :], in_=ps[6][:], func=AF.Sqrt, bias=epsT[:], scale=1.0 / D),
                     reads=[('ps', 6)], writes=[('rstd', par)])
                P.op('dve', lambda e: e.reciprocal(out=rstd[:, par, :], in_=rstd[:, par, :]),
                     reads=[('rstd', par)], writes=[('rstd', par)])

            def load_chunk(tt, c):
                P.dma('sp', xt[:, c, :], xT[c * 128:(c + 1) * 128, tt * T:(tt + 1) * T], reads=[('xT', tt, c)],
                      writes=[('xt', c)])

            def make_h(par, c):
                P.op('dve', (lambda e: e.scalar_tensor_tensor(
                    out=hT[:, par, c, :], in0=xt[:, c, :], scalar=smalls[:, gbase + c:gbase + c + 1], in1=rstd[:, par, :],
                    op0=ALU.mult, op1=ALU.mult)),
                    reads=[('xt', c), ('rstd', par)], writes=[('hT', par, c)])

            ensure_gu(0)
            for c in range(KC):
                load_chunk(0, c)
            for c in range(KC):
                s_ = c % 2
                if c == 0:
                    P.op('act', (lambda e: e.activation(out=acc[:], in_=xt[:, 0, :], func=AF.Square)),
                         reads=[('xt', 0)], writes=['acc'])
                else:
                    P.op('act', (lambda e, c=c, s_=s_: e.activation(out=xr[:, s_, :], in_=xt[:, c, :], func=AF.Square)),
                         reads=[('xt', c)], writes=[('xr', s_)])
                    P.op('dve', (lambda e, s_=s_: e.tensor_tensor(out=acc[:], in0=acc[:], in1=xr[:, s_, :], op=ALU.add)),
                         reads=[('xr', s_), 'acc'], writes=['acc'])
            ssq_finish(0)
            for c in range(KC):
                make_h(0, c)
            for tt in range(NT):
                par = tt % 2
                more = tt + 1 < NT
                for g in range(22):
                    q = tt * 22 + g
                    ensure_gu(q + 1)
                    s = q % 2
                    if g == 1 and more:
                        ssq_load(tt + 1, 0)
                    if g == 12 and more:
                        ssq_finish(1 - par)
                    if g == 20:
                        ensure_d(tt * 8)
                    for j in range(2):
                        f = 2 * g + j
                        bG = f % 2
                        bU = 2 + f % 2
                        ci = f - 4
                        if more and 0 <= ci < KC:
                            if ci + 1 < KC:
                                ssq_load(tt + 1, ci + 1)
                            ssq_chunk(tt + 1, ci)
                        for k in range(KC):
                            mm(ps[bG][:], wgb[:, s, k, j * 128:(j + 1) * 128], hT[:, par, k, :], k == 0, k == KC - 1,
                               reads=[('wgb', s), ('hT', par, k)], writes=[('ps', bG)], tick=(k == KC - 1))
                        for k in range(KC):
                            mm(ps[bU][:], wub[:, s, k, j * 128:(j + 1) * 128], hT[:, par, k, :], k == 0, k == KC - 1,
                               reads=[('wub', s)], writes=[('ps', bU)], tick=(k == KC - 1))
                        P.op('act', (lambda e, f=f, bG=bG: e.activation(out=sg[:, f % 2, :], in_=ps[bG][:], func=AF.Silu)),
                             reads=[('ps', bG)], writes=[('sg', f % 2)])
                        P.op('dve', (lambda e, f=f, bU=bU: e.tensor_tensor(out=actT[:, f, :], in0=ps[bU][:], in1=sg[:, f % 2, :],
                                                                           op=ALU.mult)),
                             reads=[('ps', bU), ('sg', f % 2)], writes=[('actT', f)])
                akeys = [('actT', f) for f in range(FC)]
                for g in range(8):
                    q = tt * 8 + g
                    ensure_d(q + 1)
                    s = q % 2
                    if g == 6 and more:
                        ensure_gu((tt + 1) * 22)
                    for j in range(2):
                        o = 2 * g + j
                        bY = 4 + o % 2
                        for k in range(FC):
                            mm(ps[bY][:], wdb[:, s, k, j * 128:(j + 1) * 128], actT[:, k, :], k == 0, k == FC - 1,
                               reads=[('wdb', s, 0), ('wdb', s, 1)] + (akeys if k == 0 else []), writes=[('ps', bY)],
                               tick=(k == FC - 1))
                        P.op('dve', (lambda e, o=o, bY=bY: e.scalar_tensor_tensor(
                            out=xt[:, o, :], in0=ps[bY][:], scalar=0.5, in1=xt[:, o, :], op0=ALU.mult, op1=ALU.add)),
                            reads=[('ps', bY), ('xt', o)], writes=[('xt', o)])
                        P.dma('sp', xT[o * 128:(o + 1) * 128, tt * T:(tt + 1) * T], xt[:, o, :], reads=[('xt', o)],
                              writes=[('xT', tt, o)])
                        if more:
                            load_chunk(tt + 1, o)
                            if o > 0:
                                make_h(1 - par, o - 1)
                if more:
                    make_h(1 - par, KC - 1)
            P.barrier()

    def phase_m1(l):
        gbase = SM['MX'] + l * 16
        with ExitStack() as ph:
            xt = sb("xt1", [128, KC, T], F32, ph)
            hT2 = sb("hT1", [128, 2, KC, T], BF16, ph)
            wb = sb("wb1", [128, 3, 16, 256], BF16, ph)
            wfl = sb("wfl", [128, 16, 4], BF16, ph)
            wuq = sb("wuq", [128, 4, 2048], BF16, ph)
            wukv = sb("wukv", [128, 4, 2048], BF16, ph)
            sq = sb("sq1", [128, 2, T], F32, ph)
            acc = sb("acc1", [128, T], F32, ph)
            acc2 = sb("acc2", [128, T], F32, ph)
            acc3 = sb("acc3", [128, T], F32, ph)
            rstd3 = sb("rstd3", [128, T], F32, ph)
            rstd = sb("rstd1", [128, T], F32, ph)
            rstd2 = sb("rstd2", [128, T], F32, ph)
            cq = sb("cq", [128, 4, T], F32, ph)
            ckv = sb("ckv", [128, 4, T], F32, ph)
            cqn = sb("cqn", [128, 4, T], BF16, ph)
            ckvn = sb("ckvn", [128, 4, T], BF16, ph)
            rC = sb("rC", [128, 2, T], F32, ph)
            rM = sb("rM", [128, 2, T], F32, ph)
            tA = sb("tA", [128, T], F32, ph)
            tB = sb("tB", [128, T], F32, ph)
            stg = sb("stg1", [128, 4, T], BF16, ph)
            vst = sb("vst", [128, 4, 2048], BF16, ph)
            flg = sb("flg", [4, T], F32, ph)
            P.dma('sp', wfl[:], wfl_s[l], reads=[('wfl_s', l)], writes=['wfl'])
            P.dma('sp', wuq[:], wuq_s[l], reads=[('wuq_s', l)], writes=['wuq'])
            P.dma('sp', wukv[:], wukv_s[l], reads=[('wukv_s', l)], writes=['wukv'])
            bank_rr = [0]
            stg_rr = [0]
            nxt = [0]

            def ensure_loaded(q):
                q = min(q, NT * 21 - 1)
                while nxt[0] <= q:
                    g_ = nxt[0] % 21
                    P.dma('sp', wb[:, g_ % 3], win_s[l][g_], reads=[('win_s', l, g_)], writes=[('wb', g_ % 3)])
                    nxt[0] += 1

            def nbank():
                b = bank_rr[0] % 6
                bank_rr[0] += 1
                return b

            def store_fm(src_ps_bank, dst_chunk, tt, pre=None):
                s = stg_rr[0] % 4
                stg_rr[0] += 1
                if pre is None:
                    copy_any(stg[:, s, :], ps[src_ps_bank][:], [('ps', src_ps_bank)], [('stg', s)])
                else:
                    pre(stg[:, s, :], ('stg', s))
                P.dma('sp', fmS[dst_chunk][:, tt * T:(tt + 1) * T], stg[:, s, :], reads=[('stg', s)],
                      writes=[('fmS', dst_chunk, tt)])

            def rope_pair(bA, bB, tab, tabkey, dst_chunk, tt):
                P.op('dve', lambda e: e.tensor_tensor(out=tA[:], in0=ps[bA][:], in1=tab[:, 0, :], op=ALU.mult),
                     reads=[('ps', bA), tabkey], writes=['tA'])
                P.op('dve', lambda e: e.tensor_tensor(out=tB[:], in0=ps[bB][:], in1=tab[:, 1, :], op=ALU.mult),
                     reads=[('ps', bB), tabkey], writes=['tB'])

                def pre(dst, dkey):
                    P.op('dve', lambda e: e.tensor_tensor(out=dst, in0=tA[:], in1=tB[:], op=ALU.add),
                         reads=['tA', 'tB'], writes=[dkey])
                store_fm(None, dst_chunk, tt, pre=pre)

            load_xt(xt, 0)
            ensure_loaded(1)

            def xnorm_p1():
                rms_p1(lambda c: xt[:, c, :], KC, lambda c: ('xt', c), sq, acc, 'acc')

            def xnorm_p2(par):
                rms_p2(lambda c: xt[:, c, :], KC, lambda c: smalls[:, gbase + c:gbase + c + 1],
                       lambda c: hT2[:, par, c, :], D, lambda c: ('xt', c), lambda c: ('hT', par, c), rstd, 6, 'rstd',
                       acc, 'acc')

            xnorm_p1()
            xnorm_p2(0)
            if NT > 1:
                load_xt(xt, 1)
            qn_base = SM['QN'] + l * 4
            kvn_base = SM['KVN'] + l * 4
            for tt in range(NT):
                par = tt % 2
                hT = hT2[:, par]
                P.dma('sp', rC[:], ropeC_in[:, :, tt * T:(tt + 1) * T].rearrange("a p t -> p a t"), writes=['rC'])
                P.dma('sp', rM[:], ropeM_in[:, :, tt * T:(tt + 1) * T].rearrange("a p t -> p a t"), writes=['rM'])
                hkeys = [('hT', par, c) for c in range(KC)]
                pend = None
                for g in range(17):
                    s = g % 3
                    ensure_loaded(tt * 21 + g + 2)
                    banks = []
                    for j in range(2):
                        b = nbank()
                        banks.append(b)
                        for k in range(KC):
                            mm(ps[b][:], wb[:, s, k, j * 128:(j + 1) * 128], hT[:, k, :], k == 0, k == KC - 1,
                               reads=[('wb', s)] + (hkeys if k == 0 else []), writes=[('ps', b)], tick=(k == KC - 1))
                    for j in range(2):
                        cidx = 2 * g + j
                        b = banks[j]
                        if cidx < 8:
                            store_fm(b, cidx, tt)
                        elif cidx < 12:
                            copy_any(cq[:, cidx - 8, :], ps[b][:], [('ps', b)], [('cq', cidx - 8)])
                        elif cidx < 16:
                            copy_any(ckv[:, cidx - 12, :], ps[b][:], [('ps', b)], [('ckv', cidx - 12)])
                    if g == 6:
                        rms_p1(lambda c: cq[:, c, :], 4, lambda c: ('cq', c), sq, acc2, 'acc2')
                    if g == 8:
                        rms_p2(lambda c: cq[:, c, :], 4, lambda c: smalls[:, qn_base + c:qn_base + c + 1],
                               lambda c: cqn[:, c, :], 512, lambda c: ('cq', c), lambda c: ('cqn', c), rstd2, 7, 'rstd2',
                               acc2, 'acc2')
                        rms_p1(lambda c: ckv[:, c, :], 4, lambda c: ('ckv', c), sq, acc3, 'acc3')
                    if g == 10:
                        rms_p2(lambda c: ckv[:, c, :], 4, lambda c: smalls[:, kvn_base + c:kvn_base + c + 1],
                               lambda c: ckvn[:, c, :], 512, lambda c: ('ckv', c), lambda c: ('ckvn', c), rstd3, 7, 'rstd3',
                               acc3, 'acc3')
                    if g == 12 and tt + 1 < NT:
                        xnorm_p1()
                    if g == 8:
                        rope_pair(banks[0], banks[1], rM, 'rM', 28, tt)
                    elif 9 <= g <= 12:
                        rope_pair(banks[0], banks[1], rC, 'rC', 29 + (g - 9), tt)
                    elif g >= 13:
                        rope_pair(banks[0], banks[1], rC, 'rC', 33 + (g - 13), tt)
                b = nbank()
                for k in range(KC):
                    mm(ps[b][0:4, :], wfl[:, k, :], hT[:, k, :], k == 0, k == KC - 1,
                       reads=['wfl'] + (hkeys if k == 0 else []), writes=[('ps', b)], tick=(k == KC - 1))
                copy_any(flg[:], ps[b][0:4, :], [('ps', b)], ['flg'])
                P.dma('sp', flogS[:, tt * T:(tt + 1) * T], flg[:], reads=['flg'], writes=[('flogS', tt)])
                for vi, (g0, col0) in enumerate(((17, 0), (19, 1536))):
                    ensure_loaded(tt * 21 + g0 + 2)
                    for blk in range(4):
                        b = nbank()
                        for hf in range(2):
                            s = (g0 + hf) % 3
                            for k in range(KC):
                                mm(ps[b][:, hf * 256:(hf + 1) * 256], hT[:, k, blk * 128:(blk + 1) * 128], wb[:, s, k, :],
                                   k == 0, k == KC - 1, reads=[('wb', s)] + (hkeys if k == 0 else []),
                                   writes=[('ps', b)], tick=(k == KC - 1))
                        copy_any(vst[:, blk, col0:col0 + 512], ps[b][:], [('ps', b)], [('vst', blk, vi)])
                if tt + 1 < NT:
                    xnorm_p2(1 - par)
                    if tt + 2 < NT:
                        load_xt(xt, tt + 2)
                qkeys = [('cqn', c) for c in range(4)]
                for h in range(8):
                    b = nbank()
                    for k in range(4):
                        mm(ps[b][:], wuq[:, k, h * 128:(h + 1) * 128], cqn[:, k, :], k == 0, k == 3,
                           reads=['wuq'] + (qkeys if k == 0 else []), writes=[('ps', b)], tick=(k == 3))
                    store_fm(b, 8 + h, tt)
                for i in range(4):
                    bA = nbank()
                    bB = nbank()
                    for k in range(4):
                        mm(ps[bA][:], wuq[:, k, 1024 + i * 128:1024 + (i + 1) * 128], cqn[:, k, :], k == 0, k == 3,
                           reads=['wuq'] + (qkeys if k == 0 else []), writes=[('ps', bA)], tick=(k == 3))
                    for k in range(4):
                        mm(ps[bB][:], wuq[:, k, 1536 + i * 128:1536 + (i + 1) * 128], cqn[:, k, :], k == 0, k == 3,
                           reads=['wuq'], writes=[('ps', bB)], tick=(k == 3))
                    rope_pair(bA, bB, rM, 'rM', 24 + i, tt)
                kkeys = [('ckvn', c) for c in range(4)]
                for h in range(8):
                    b = nbank()
                    for k in range(4):
                        mm(ps[b][:], wukv[:, k, h * 128:(h + 1) * 128], ckvn[:, k, :], k == 0, k == 3,
                           reads=['wukv'] + (kkeys if k == 0 else []), writes=[('ps', b)], tick=(k == 3))
                    store_fm(b, 16 + h, tt)
                for blk in range(4):
                    for hf in range(2):
                        b = nbank()
                        for k in range(4):
                            mm(ps[b][:], ckvn[:, k, blk * 128:(blk + 1) * 128], wukv[:, k, 1024 + hf * 512:1024 + (hf + 1) * 512],
                               k == 0, k == 3, reads=['wukv'] + (kkeys if k == 0 else []), writes=[('ps', b)], tick=(k == 3))
                        copy_any(vst[:, blk, 512 + hf * 512:512 + (hf + 1) * 512], ps[b][:], [('ps', b)],
                                 [('vst', blk, 2 + hf)])
                P.dma('sp', vS[tt * T:(tt + 1) * T, :].rearrange("(b p) c -> p b c", p=128), vst[:],
                      reads=[('vst', blk, i) for blk in range(4) for i in range(4)], writes=[('vS', tt)])
            P.barrier()

    def phase_m2(l):
        with ExitStack() as ph:
            fl = sb("fl", [4, S], F32, ph)
            z = sb("z", [4, S], F32, ph)
            cs = sb("cs", [4, S], F32, ph)
            on4 = sb("on4", [4, S], F32, ph)
            pb = sb("pb", [4, 3, S], BF16, ph)
            nb_ = sb("nb_", [4, 3, S], BF16, ph)
            pf = sb("pf", [4, S], F32, ph)
            P.dma('sp', fl[:], flogS, writes=['fl'])
            P.op('dve', lambda e: e.memset(on4[:], 1.0), writes=['on4'])
            P.op('act', lambda e: e.activation(out=z[:], in_=fl[:], func=AF.Exp, bias=negb[:, l:l + 1], scale=-1.0),
                 reads=['fl'], writes=['z'])
            P.op('act', lambda e: e.activation(out=z[:], in_=z[:], func=AF.Ln, bias=one1[0:4, :], scale=1.0),
                 reads=['z'], writes=['z'])
            P.op('dve', lambda e: e.tensor_tensor_scan(out=cs[:], data0=on4[:], data1=z[:], initial=0.0,
                                                       op0=ALU.mult, op1=ALU.add), reads=['on4', 'z'], writes=['cs'])
            P.op('dve', lambda e: e.tensor_scalar_mul(out=cs[:], in0=cs[:], scalar1=float(np.sqrt(128.0))),
                 reads=['cs'], writes=['cs'])
            for i in range(3):
                P.op('dve', (lambda e, i=i: e.tensor_copy(out=pb[:, i, :], in_=cs[:])), reads=['cs'], writes=[('pb', i)])
                P.op('dve', (lambda e, i=i: e.tensor_scalar_mul(out=nb_[:, i, :], in0=pb[:, i, :], scalar1=-1.0)),
                     reads=[('pb', i)], writes=[('nb', i)])
                if i < 2:
                    P.op('dve', (lambda e, i=i: e.tensor_copy(out=pf[:], in_=pb[:, i, :])), reads=[('pb', i)], writes=['pf'])
                    P.op('dve', lambda e: e.tensor_tensor(out=cs[:], in0=cs[:], in1=pf[:], op=ALU.subtract),
                         reads=['cs', 'pf'], writes=['cs'])
            P.dma('sp', cpS[0].rearrange("i h s -> h i s"), pb[:], reads=[('pb', i) for i in range(3)], writes=['cpS0'])
            P.dma('sp', cpS[1].rearrange("i h s -> h i s"), nb_[:], reads=[('nb', i) for i in range(3)], writes=['cpS1'])
            P.barrier()

    def phase_m3(l):
        with ExitStack() as ph:
            qT = sb("qT", [128, 2, S], BF16, ph)
            kT = sb("kT", [128, 2, S], BF16, ph)
            qx = sb("qx", [128, 2, S], BF16, ph)
            kx = sb("kx", [128, 2, S], BF16, ph)
            krT = sb("krT", [128, S], BF16, ph)
            vt = sb("vt", [128, 2, NB, 128], BF16, ph)
            pT = sb("pT", [128, 3, T], BF16, ph)
            rden = sb("rden", [128, T], F32, ph)
            ostg = sb("ostg", [128, 2, T], BF16, ph)
            o_rr = [0]
            hslot = [0]
            P.op('dve', lambda e: e.memset(qx[:, :, :], 0.0), writes=[('qx', 0), ('qx', 1)])
            P.op('dve', lambda e: e.memset(kx[:, :, :], 0.0), writes=[('kx', 0), ('kx', 1)])
            P.op('dve', lambda e: e.memset(krT[:, :], 0.0), writes=['krT'])
            P.op('dve', lambda e: e.memset(qx[0:6, :, :], 1.0), writes=[('qx', 0), ('qx', 1)])
            P.op('dve', lambda e: e.memset(kx[0:6, :, :], 1.0), writes=[('kx', 0), ('kx', 1)])

            def finalize(j, chunk):
                bO = 3 + j % 2
                bD = 5 + j % 2
                s = o_rr[0] % 2
                o_rr[0] += 1
                P.op('dve', lambda e: e.reciprocal(out=rden[:], in_=ps[bD][:]), reads=[('ps', bD)], writes=['rden'])
                P.op('dve', lambda e: e.tensor_tensor(out=ostg[:, s, :], in0=ps[bO][:], in1=rden[:], op=ALU.mult),
                     reads=[('ps', bO), 'rden'], writes=[('ostg', s)])
                P.dma('sp', mixT[chunk][:, j * T:(j + 1) * T], ostg[:, s, :], reads=[('ostg', s)],
                      writes=[('mixT', chunk, j)])

            def causal_head(hs, scale, extra, chunk):
                tiles = [(j, kb) for j in range(NT) for kb in range(4 * j + 4)]
                n = len(tiles)
                rk = [('qT', hs), ('kT', hs), ('qx', hs), ('kx', hs), 'krT']

                def geom(i):
                    j, kb = tiles[i]
                    a = kb - 4 * j
                    c0 = 128 * a if a > 0 else 0
                    return j, kb, a, c0

                def emitQK(i):
                    j, kb, a, c0 = geom(i)
                    s = i % 3
                    mm(ps[s][:, c0:T], kT[:, hs, kb * 128:(kb + 1) * 128], qT[:, hs, j * T + c0:(j + 1) * T], True,
                       extra is None, reads=rk, writes=[('ps', s)], tick=(extra is None))
                    if extra is not None:
                        kr_, lt, rt = extra
                        mm(ps[s][:, c0:T], lt[0:kr_, kb * 128:(kb + 1) * 128], rt[0:kr_, j * T + c0:(j + 1) * T], False, True,
                           reads=rk, writes=[('ps', s)], tick=True)
                    P.op('act', (lambda e: e.activation(out=pT[:, s, c0:T], in_=ps[s][:, c0:T], func=AF.Exp, scale=scale)),
                         reads=[('ps', s)], writes=[('pT', s)])
                    if a >= 0:
                        P.op('dve', (lambda e: e.tensor_tensor(out=pT[:, s, c0:c0 + 128], in0=pT[:, s, c0:c0 + 128],
                                                               in1=tri_ge, op=ALU.mult)),
                             reads=[('pT', s)], writes=[('pT', s)])

                def emitPV(i):
                    j, kb, a, c0 = geom(i)
                    s = i % 3
                    first = (kb == 0)
                    last = (kb == 4 * j + 3)
                    bO = 3 + j % 2
                    bD = 5 + j % 2
                    mm(ps[bO][:, c0:T], vt[:, hs, kb, :], pT[:, s, c0:T], first, last,
                       reads=[('pT', s), ('vt', hs)], writes=[('ps', bO)], tick=False)
                    mm(ps[bD][:, c0:T], ones_b[:], pT[:, s, c0:T], first, last,
                       reads=[('pT', s)], writes=[('ps', bD)], tick=True)
                    if last:
                        finalize(j, chunk)

                for i in range(n + 2):
                    if i < n:
                        emitQK(i)
                    if i >= 2:
                        emitPV(i - 2)

            for h in range(4):
                hs = hslot[0] % 2
                hslot[0] += 1
                P.dma('sp', qT[:, hs, :], fmS[h], reads=[('fmS', h, tt) for tt in range(NT)], writes=[('qT', hs)])
                P.dma('sp', kT[:, hs, :], fmS[4 + h], reads=[('fmS', 4 + h, tt) for tt in range(NT)], writes=[('kT', hs)])
                P.dma('sp', kx[0:3, hs, :], cpS[0, :, h, :], reads=['cpS0'], writes=[('kx', hs)])
                P.dma('sp', qx[3:6, hs, :], cpS[1, :, h, :], reads=['cpS1'], writes=[('qx', hs)])
                P.dma('sp', vt[:, hs], vS[:, h * 128:(h + 1) * 128].rearrange("(b p) c -> p b c", p=128),
                      reads=[('vS', tt) for tt in range(NT)], writes=[('vt', hs)])
                causal_head(hs, float(128.0 ** -0.5), (128, kx[:, hs, :], qx[:, hs, :]), h)
            P.barrier()
            P.dma('sp', krT[0:64, :], fmS[28][0:64, :], reads=[('fmS', 28, tt) for tt in range(NT)], writes=['krT'])
            for h in range(8):
                hs = hslot[0] % 2
                hslot[0] += 1
                P.dma('sp', qT[:, hs, :], fmS[8 + h], reads=[('fmS', 8 + h, tt) for tt in range(NT)], writes=[('qT', hs)])
                P.dma('sp', kT[:, hs, :], fmS[16 + h], reads=[('fmS', 16 + h, tt) for tt in range(NT)], writes=[('kT', hs)])
                P.dma('sp', qx[0:64, hs, :], fmS[24 + h // 2][(h % 2) * 64:(h % 2) * 64 + 64, :],
                      reads=[('fmS', 24 + h // 2, tt) for tt in range(NT)], writes=[('qx', hs)])
                P.dma('sp', vt[:, hs], vS[:, 512 + h * 128:512 + (h + 1) * 128].rearrange("(b p) c -> p b c", p=128),
                      reads=[('vS', tt) for tt in range(NT)], writes=[('vt', hs)])
                causal_head_extra = (128, krT[:, :], qx[:, hs, :])
                causal_head(hs, float(192.0 ** -0.5), causal_head_extra, 4 + h)
            P.barrier()

    def phase_m3c(l):
        with ExitStack() as ph:
            qT2 = sb("cqT", [128, 2, S], BF16, ph)
            kT2 = sb("ckT", [128, 2, S], BF16, ph)
            qdb = {4: sb("qd4", [128, 2, S], BF16, ph), 16: sb("qd16", [128, 2, S], BF16, ph)}
            kdb = {4: sb("kd4", [128, 2, S], BF16, ph), 16: sb("kd16", [128, 2, S], BF16, ph)}
            vdb = {1: sb("vd1", [128, 2, S], BF16, ph), 4: sb("vd4", [128, 2, S], BF16, ph),
                   16: sb("vd16", [128, 2, S], BF16, ph)}
            num = sb("num", [128, S], F32, ph)
            den = sb("den", [128, S], F32, ph)
            pT = sb("cpT", [128, 3, 512], BF16, ph)
            rden = sb("crden", [128, T], F32, ph)
            ostg = sb("costg", [128, 2, T], BF16, ph)
            scale = float(128.0 ** -0.5)
            it = [0]
            grp = [0]
            DIL = (1, 4, 16)

            def prep_loads(h):
                hp = h % 2
                col = 1536 + h * 128
                P.dma('sp', qT2[:, hp, :], fmS[29 + h], reads=[('fmS', 29 + h, tt) for tt in range(NT)], writes=[('qT', hp)])
                P.dma('sp', kT2[:, hp, :], fmS[33 + h], reads=[('fmS', 33 + h, tt) for tt in range(NT)], writes=[('kT', hp)])
                for d in DIL:
                    for r in range(d):
                        P.dma('sp', vdb[d][:, hp, :].rearrange("p (r kb c) -> p r kb c", r=d, c=128)[:, r],
                              vS[:, col:col + 128].rearrange("(kb p r) c -> r p kb c", p=128, r=d)[r],
                              reads=[('vS', tt) for tt in range(NT)], writes=[('vd', d, hp)])

            def prep_copies(h):
                hp = h % 2
                for d in (4, 16):
                    P.op('dve', (lambda e, d=d: e.tensor_copy(out=qdb[d][:, hp, :].rearrange("p (r m) -> p r m", r=d),
                                                              in_=qT2[:, hp, :].rearrange("p (m r) -> p r m", r=d))),
                         reads=[('qT', hp)], writes=[('qd', d, hp)])
                    if d == 4:
                        P.op('dve', (lambda e, d=d: e.tensor_copy(out=kdb[d][:, hp, :].rearrange("p (r m) -> p r m", r=d),
                                                                  in_=kT2[:, hp, :].rearrange("p (m r) -> p r m", r=d))),
                             reads=[('kT', hp)], writes=[('kd', d, hp)])
                    else:
                        P.op('act', (lambda e, d=d: e.copy(out=kdb[d][:, hp, :].rearrange("p (r m) -> p r m", r=d),
                                                           in_=kT2[:, hp, :].rearrange("p (m r) -> p r m", r=d))),
                             reads=[('kT', hp)], writes=[('kd', d, hp)])

            def qk_stage(st):
                (bi, d, Lr, qsrc, ksrc, vview, qk_keys, vkey, r, n0, ng, nn, nb2, g_i, i) = st
                s = i % 3
                base = r * Lr
                c_lo = 0 if (n0 + nn) > 0 else 128
                c_hi = 256 * nb2
                for u in range(nb2):
                    n_ = n0 + nn + u
                    q_ap = qsrc[:, base + n_ * 128: base + (n_ + 1) * 128]
                    lastu = (u == nb2 - 1)
                    if n_ > 0:
                        mm(ps[s][:, u * 256:u * 256 + 128], ksrc[:, base + (n_ - 1) * 128: base + n_ * 128], q_ap, True, True,
                           reads=qk_keys, writes=[('ps', s)], tick=False)
                    mm(ps[s][:, u * 256 + 128:u * 256 + 256], ksrc[:, base + n_ * 128: base + (n_ + 1) * 128], q_ap, True, True,
                       reads=qk_keys, writes=[('ps', s)], tick=lastu)
                P.op('act', (lambda e: e.activation(out=pT[:, s, c_lo:c_hi], in_=ps[s][:, c_lo:c_hi], func=AF.Exp, scale=scale)),
                     reads=[('ps', s)], writes=[('pT', s)])
                P.op('dve', (lambda e: e.tensor_tensor(out=pT[:, s, c_lo:c_hi], in0=pT[:, s, c_lo:c_hi],
                                                       in1=mask4[:, c_lo:c_hi], op=ALU.mult)),
                     reads=[('pT', s)], writes=[('pT', s)])

            def pv_stage(st):
                (bi, d, Lr, qsrc, ksrc, vview, qk_keys, vkey, r, n0, ng, nn, nb2, g_i, i) = st
                s = i % 3
                bO = 3 + g_i % 2
                bD = 5 + g_i % 2
                for u in range(nb2):
                    n_ = n0 + nn + u
                    oc = slice((nn + u) * 128, (nn + u + 1) * 128)
                    lastu = (u == nb2 - 1)
                    if n_ > 0:
                        mm(ps[bO][:, oc], vview[:, r, n_ - 1, :], pT[:, s, u * 256:u * 256 + 128], True, False,
                           reads=[('pT', s), vkey], writes=[('ps', bO)], tick=False)
                        mm(ps[bD][:, oc], ones_b[:], pT[:, s, u * 256:u * 256 + 128], True, False,
                           reads=[('pT', s)], writes=[('ps', bD)], tick=False)
                    mm(ps[bO][:, oc], vview[:, r, n_, :], pT[:, s, u * 256 + 128:u * 256 + 256], n_ == 0, True,
                       reads=[('pT', s), vkey], writes=[('ps', bO)], tick=False)
                    mm(ps[bD][:, oc], ones_b[:], pT[:, s, u * 256 + 128:u * 256 + 256], n_ == 0, True,
                       reads=[('pT', s)], writes=[('ps', bD)], tick=lastu)
                if nn + nb2 == ng:
                    nview = num[:].rearrange("p (m r) -> p r m", r=d)[:, r, n0 * 128:(n0 + ng) * 128]
                    dview = den[:].rearrange("p (m r) -> p r m", r=d)[:, r, n0 * 128:(n0 + ng) * 128]
                    w_ = ng * 128
                    if bi == 0:
                        P.op('act', (lambda e: e.copy(out=nview, in_=ps[bO][:, 0:w_])), reads=[('ps', bO)], writes=['num'])
                        P.op('dve', (lambda e: e.tensor_copy(out=dview, in_=ps[bD][:, 0:w_])), reads=[('ps', bD)], writes=['den'])
                    else:
                        P.op('dve', (lambda e: e.tensor_tensor(out=nview, in0=nview, in1=ps[bO][:, 0:w_], op=ALU.add)),
                             reads=[('ps', bO), 'num'], writes=['num'])
                        P.op('dve', (lambda e: e.tensor_tensor(out=dview, in0=dview, in1=ps[bD][:, 0:w_], op=ALU.add)),
                             reads=[('ps', bD), 'den'], writes=['den'])

            def head_steps(h):
                hp = h % 2
                steps = []
                for bi, d in enumerate(DIL):
                    Lr = S // d
                    nblk = Lr // 128
                    if d == 1:
                        qsrc, ksrc = qT2[:, hp, :], kT2[:, hp, :]
                        qk_keys = [('qT', hp), ('kT', hp)]
                    else:
                        qsrc, ksrc = qdb[d][:, hp, :], kdb[d][:, hp, :]
                        qk_keys = [('qd', d, hp), ('kd', d, hp)]
                    vview = vdb[d][:, hp, :].rearrange("p (r kb c) -> p r kb c", r=d, c=128)
                    vkey = ('vd', d, hp)
                    for r in range(d):
                        for n0 in range(0, nblk, 4):
                            ng = min(4, nblk - n0)
                            g_i = grp[0]
                            grp[0] += 1
                            for nn in range(0, ng, 2):
                                nb2 = min(2, ng - nn)
                                steps.append((bi, d, Lr, qsrc, ksrc, vview, qk_keys, vkey, r, n0, ng, nn, nb2, g_i, it[0]))
                                it[0] += 1
                return steps

            def finalize(h):
                for j in range(NT):
                    s = j % 2
                    P.op('dve', (lambda e, j=j: e.reciprocal(out=rden[:], in_=den[:, j * T:(j + 1) * T])),
                         reads=['den'], writes=['rden'])
                    P.op('dve', (lambda e, j=j, s=s: e.tensor_tensor(out=ostg[:, s, :], in0=num[:, j * T:(j + 1) * T], in1=rden[:],
                                                                     op=ALU.mult)),
                         reads=['num', 'rden'], writes=[('ostg', s)])
                    P.dma('sp', mixT[12 + h][:, j * T:(j + 1) * T], ostg[:, s, :], reads=[('ostg', s)],
                          writes=[('mixT', 12 + h, j)])

            prep_loads(0)
            prep_copies(0)
            for h in range(4):
                if h + 1 < 4:
                    prep_loads(h + 1)
                steps = head_steps(h)
                nst = len(steps)
                mid = nst // 3
                for i_ in range(nst + 2):
                    if i_ == mid and h + 1 < 4:
                        prep_copies(h + 1)
                    if i_ < nst:
                        qk_stage(steps[i_])
                    if i_ >= 2:
                        pv_stage(steps[i_ - 2])
                finalize(h)
            P.barrier()

    def phase_m4(l):
        with ExitStack() as ph:
            xt2 = sb("xt4", [128, 2, KC, T], F32, ph)
            mt2 = sb("mt", [128, 2, KC, T], BF16, ph)
            wob = sb("wob", [128, 8, 16, 256], BF16, ph)

            def loads(tt):
                s_ = tt % 2
                P.dma('sp', mt2[:, s_], mixT[:, :, tt * T:(tt + 1) * T].rearrange("c p t -> p c t"), writes=[('mt', s_)])
                P.dma('sp', xt2[:, s_], xT[:, tt * T:(tt + 1) * T].rearrange("(c p) t -> p c t", p=128),
                      reads=[('xT', tt)], writes=[('xt', s_, c) for c in range(KC)])

            P.dma('sp', wob[:, 0], wout_s[l][0], reads=[('wout_s', l, 0)], writes=[('wob', 0)])
            loads(0)
            for g in range(1, 8):
                P.dma('sp', wob[:, g], wout_s[l][g], reads=[('wout_s', l, g)], writes=[('wob', g)])
            for tt in range(NT):
                s_ = tt % 2
                if tt + 1 < NT:
                    loads(tt + 1)
                for g in range(8):
                    for j in range(2):
                        o = 2 * g + j
                        bY = o % 4
                        for k in range(KC):
                            mm(ps[bY][:], wob[:, g, k, j * 128:(j + 1) * 128], mt2[:, s_, k, :], k == 0, k == KC - 1,
                               reads=[('wob', g), ('mt', s_)], writes=[('ps', bY)], tick=(k == KC - 1))
                        P.op('dve', (lambda e, o=o, bY=bY, s_=s_: e.tensor_tensor(out=xt2[:, s_, o, :], in0=ps[bY][:],
                                                                                 in1=xt2[:, s_, o, :], op=ALU.add)),
                             reads=[('ps', bY), ('xt', s_, o)], writes=[('xt', s_, o)])
                P.dma('sp', xT[:, tt * T:(tt + 1) * T].rearrange("(c p) t -> p c t", p=128), xt2[:, s_],
                      reads=[('xt', s_, c) for c in range(KC)], writes=[('xT', tt)])
            P.barrier()

    def phase_final():
        gbase = SM['FIN']
        with ExitStack() as ph:
            xt = sb("xtf", [128, KC, T], F32, ph)
            sq = sb("sqf", [128, 2, T], F32, ph)
            acc = sb("accf", [128, T], F32, ph)
            rstd = sb("rstdf", [128, T], F32, ph)
            ost = sb("ost", [128, 4, D], F32, ph)
            for tt in range(NT):
                load_xt(xt, tt)
                rmsnorm(lambda c: xt[:, c, :], KC, lambda c: smalls[:, gbase + c:gbase + c + 1], lambda c: xt[:, c, :],
                        D, lambda c: ('xt', c), lambda c: ('xt', c), sq, rstd, 6, 'rstd', acc)
                i = 0
                for b in range(4):
                    for c4 in range(4):
                        bank = i % 4
                        i += 1
                        for cc in range(4):
                            c = c4 * 4 + cc
                            o_ap = ps[bank][:, cc * 128:(cc + 1) * 128]
                            i_ap = xt[:, c, b * 128:(b + 1) * 128]
                            P.op('pe', (lambda e, o_ap=o_ap, i_ap=i_ap: e.transpose(out=o_ap, in_=i_ap, identity=ident)),
                                 reads=[('xt', c)], writes=[('ps', bank)], tick=(cc == 3))
                        copy_any(ost[:, b, c4 * 512:(c4 + 1) * 512], ps[bank][:], [('ps', bank)], [('ost', b, c4)])
                P.dma('sp', out[tt * T:(tt + 1) * T, :].rearrange("(b p) d -> p b d", p=128), ost[:],
                      reads=[('ost', b, c4) for b in range(4) for c4 in range(4)], writes=[('out', tt)])
            P.barrier()

    phase_T0()
    for l in range(L):
        P.gate('pool', 'pe')
        conv_ffn(l, 1)
        phase_ffn(l, 0)
        phase_m1(l)
        phase_m2(l)
        if l + 1 < L:
            P.gate('pool', 'pe')
            conv_mix(l + 1)
            conv_ffn(l + 1, 0)
        phase_m3(l)
        phase_m3c(l)
        phase_m4(l)
        phase_ffn(l, 1)
    phase_final()
    P.replay()
    P.close()
    return nc


def _rope_tables(S):
    t = np.arange(S, dtype=np.float32)

    def tab(dim):
        inv = (1.0 / (np.float32(THETA) ** (np.arange(0, dim, 2, dtype=np.float32) / np.float32(dim)))).astype(np.float32)
        ang = (t[:, None] * inv[None, :]).astype(np.float32)
        return np.cos(ang).astype(np.float32).T, np.sin(ang).astype(np.float32).T
    cC, sC = tab(32)
    ropeC = np.zeros((2, 128, S), np.float32)
    ropeC[0, :, :] = 1.0
    ropeC[0, 0:16] = cC
    ropeC[0, 16:32] = cC
    ropeC[1, 0:16] = -sC
    ropeC[1, 16:32] = sC
    cM, sM = tab(64)
    ropeM = np.zeros((2, 128, S), np.float32)
    for hf in range(2):
        ropeM[0, hf * 64:hf * 64 + 32] = cM
        ropeM[0, hf * 64 + 32:hf * 64 + 64] = cM
        ropeM[1, hf * 64:hf * 64 + 32] = -sM
        ropeM[1, hf * 64 + 32:hf * 64 + 64] = sM
    return ropeC, ropeM


def _win_cols():
    fq, fk, fv, fl, cq, ckv, kr, dq, dk, dv = 0, 512, 1024, 1536, 1540, 2052, 2564, 2628, 3140, 3652
    cols = []
    cols += list(range(fq, fq + 512)) + list(range(fk, fk + 512)) + list(range(cq, cq + 512)) + list(range(ckv, ckv + 512))
    krc = list(range(kr, kr + 64))
    krs = list(range(kr + 32, kr + 64)) + list(range(kr, kr + 32))
    cols += krc + krc + krs + krs
    for base in (dq, dk):
        for h in range(4):
            b = base + h * 128
            cols += list(range(b, b + 128))
            cols += list(range(b + 16, b + 32)) + list(range(b, b + 16)) + list(range(b + 32, b + 128))
    cols += list(range(fv, fv + 512)) + list(range(dv, dv + 512)) + list(range(fl, fl + 4))
    assert len(cols) == NINX
    return np.array(cols)


def _wuq_cols():
    cols = []
    for h in range(8):
        cols += list(range(h * 192, h * 192 + 128))
    for h in range(8):
        cols += list(range(h * 192 + 128, h * 192 + 192))
    for h in range(8):
        cols += list(range(h * 192 + 160, h * 192 + 192)) + list(range(h * 192 + 128, h * 192 + 160))
    return np.array(cols)


def _wukv_cols():
    cols = []
    for h in range(8):
        cols += list(range(h * 256, h * 256 + 128))
    for h in range(8):
        cols += list(range(h * 256 + 128, h * 256 + 256))
    return np.array(cols)


def prep_shared(inp, L, S):
    f = lambda a: np.ascontiguousarray(np.asarray(a, dtype=np.float32))
    SM = smalls_layout(L)
    smalls = np.zeros((128, SM['N']), np.float32)
    for nm, key in (('F1', 'ffn1_norm'), ('MX', 'mix_norm'), ('F2', 'ffn2_norm')):
        g = f(inp[key])[:L]
        smalls[:, SM[nm]:SM[nm] + L * 16] = g.reshape(L, 16, 128).transpose(2, 0, 1).reshape(128, L * 16)
    smalls[:, SM['FIN']:SM['FIN'] + 16] = f(inp['final_norm']).reshape(16, 128).T
    smalls[:, SM['QN']:SM['QN'] + 4 * L] = f(inp['mla_q_norm'])[:L].reshape(L, 4, 128).transpose(2, 0, 1).reshape(128, L * 4)
    smalls[:, SM['KVN']:SM['KVN'] + 4 * L] = f(inp['mla_kv_norm'])[:L].reshape(L, 4, 128).transpose(2, 0, 1).reshape(128, L * 4)
    smalls[0:4, SM['FB']:SM['FB'] + L] = f(inp['fox_forget_bias'])[:L].T
    consts = np.zeros((128, 384), np.float32)
    consts[:, 0:128] = np.eye(128, dtype=np.float32)
    kk = np.arange(128)[:, None]
    qq = np.arange(128)[None, :]
    consts[:, 128:256] = (qq <= kk).astype(np.float32)
    consts[:, 256:384] = (qq >= kk).astype(np.float32)
    ropeC, ropeM = _rope_tables(S)
    sh = {
        "ffn1_w_gate": f(inp['ffn1_w_gate'])[:L], "ffn1_w_up": f(inp['ffn1_w_up'])[:L], "ffn1_w_down": f(inp['ffn1_w_down'])[:L],
        "ffn2_w_gate": f(inp['ffn2_w_gate'])[:L], "ffn2_w_up": f(inp['ffn2_w_up'])[:L], "ffn2_w_down": f(inp['ffn2_w_down'])[:L],
        "w_in_ext": np.ascontiguousarray(f(inp['w_in'])[:L][:, :, _win_cols()]),
        "w_uq_ext": np.ascontiguousarray(f(inp['mla_w_uq'])[:L][:, :, _wuq_cols()]),
        "w_ukv_ext": np.ascontiguousarray(f(inp['mla_w_ukv'])[:L][:, :, _wukv_cols()]),
        "w_out": f(inp['w_out'])[:L],
        "smalls": smalls, "consts": consts, "ropeC": ropeC, "ropeM": ropeM,
    }
    return sh


_NC_CACHE = {}


def kernel(**inputs):
    x = np.asarray(inputs['x'], dtype=np.float32)
    B, S, _ = x.shape
    L = int(np.asarray(inputs['w_out']).shape[0])
    key = (S, L)
    if key not in _NC_CACHE:
        _NC_CACHE[key] = build(S, L)
    nc = _NC_CACHE[key]
    sh = prep_shared(inputs, L, S)
    in_maps = []
    for b in range(B):
        m = dict(sh)
        m["x"] = np.ascontiguousarray(x[b])
        in_maps.append(m)
    res = run_bass_kernel_spmd(nc, in_maps, core_ids=list(range(B)))
    return np.stack([np.asarray(r["out"], dtype=np.float32) for r in res.results], axis=0)
```

```python
import numpy as np
from contextlib import ExitStack
import concourse.bass as bass
import concourse.mybir as mybir
from concourse.bass_utils import run_bass_kernel_spmd

F32 = mybir.dt.float32
BF16 = mybir.dt.bfloat16
AF = mybir.ActivationFunctionType
ALU = mybir.AluOpType
ENGS = ('pe', 'act', 'dve', 'pool', 'sp')

D = 2048
DFF = 5632
KC = 16
FC = 44
T = 512
EPS = 1e-6
THETA = 500000.0
NFM = 34
NINX = NFM * 128 + 1024 + 4
NFMS = 37
PERSIST = ('wg_s', 'wu_s', 'wd_s', 'win_s', 'wfl_s', 'wuq_s', 'wukv_s', 'wout_s')


class Prog:
    def __init__(self, nc, ndsem=12):
        self.nc = nc
        self.es = ExitStack()
        self.q = {e: [] for e in ENGS}
        self.semh = {}
        self.ecnt = {e: 0 for e in ENGS}
        self.seen = {e: {} for e in ENGS}
        self.lastw = {}
        self.rds = {}
        self.dcnt = {}
        self.drr = {e: 0 for e in ENGS}
        self.ndsem = ndsem
        self.pe_pending = False
        for e in ENGS:
            self.semh[('e', e)] = self.es.enter_context(nc.semaphore("s_" + e))
        for e in ('sp', 'pool'):
            for i in range(ndsem):
                k = ('d', e, i)
                self.semh[k] = self.es.enter_context(nc.semaphore("d_%s_%d" % (e, i)))
                self.dcnt[k] = 0

    def _deps(self, eng, reads, writes, is_dma):
        own = ('e', eng)
        cand = {}

        def add(t, raw):
            if t is None:
                return
            k, v = t
            if not is_dma and k == own:
                if eng == 'pe' or not raw:
                    return
            if cand.get(k, 0) < v:
                cand[k] = v
        for r in reads:
            add(self.lastw.get(r), True)
        for w in writes:
            add(self.lastw.get(w), False)
            for t in self.rds.get(w, ()):
                add(t, False)
        out = []
        seen = self.seen[eng]
        for k, v in cand.items():
            if seen.get(k, 0) >= v:
                continue
            seen[k] = v
            out.append((k, v))
        return out

    def _record(self, t, reads, writes):
        for r in reads:
            lst = self.rds.setdefault(r, [])
            if not lst or lst[-1] != t:
                lst.append(t)
        for w in writes:
            self.lastw[w] = t
            self.rds[w] = []

    def op(self, eng, fn, reads=(), writes=(), tick=True):
        waits = self._deps(eng, reads, writes, False)
        if tick:
            self.ecnt[eng] += 1
            t = (('e', eng), self.ecnt[eng])
            if eng == 'pe':
                self.pe_pending = False
        else:
            assert eng == 'pe'
            t = (('e', eng), self.ecnt[eng] + 1)
            self.pe_pending = True
        self._record(t, reads, writes)
        self.q[eng].append((waits, fn, ('e', eng) if tick else None, 1))

    def dma(self, qeng, out, in_, reads=(), writes=()):
        i = self.drr[qeng]
        self.drr[qeng] = (i + 1) % self.ndsem
        k = ('d', qeng, i)
        waits = self._deps(qeng, reads, writes, True)
        prev = self.dcnt[k]
        if prev > 0 and self.seen[qeng].get(k, 0) < prev:
            self.seen[qeng][k] = prev
            waits.append((k, prev))
        self.dcnt[k] = prev + 16
        t = (k, prev + 16)
        self._record(t, reads, writes)
        self.q[qeng].append((waits, (lambda e: e.dma_start(out=out, in_=in_)), k, 16))

    def gate(self, qeng, eng):
        v = self.ecnt[eng]
        k = ('e', eng)
        if v > 0 and self.seen[qeng].get(k, 0) < v:
            self.seen[qeng][k] = v
            self.q[qeng].append(([(k, v)], None, None, 0))

    def barrier(self):
        assert not self.pe_pending
        tickets = [(k, v) for k, v in self.dcnt.items() if v > 0 and k[1] != 'pool']
        tickets += [(('e', e), self.ecnt[e]) for e in ENGS if self.ecnt[e] > 0]
        for e in ENGS:
            seen = self.seen[e]
            w = []
            for k, v in tickets:
                if k == ('e', e):
                    continue
                if seen.get(k, 0) < v:
                    seen[k] = v
                    w.append((k, v))
            if w:
                self.q[e].append((w, None, None, 0))
        keep = {k: v for k, v in self.lastw.items() if isinstance(k, tuple) and k[0] in PERSIST}
        self.lastw.clear()
        self.lastw.update(keep)
        self.rds.clear()

    def replay(self):
        nc = self.nc
        semh = self.semh
        q = self.q

        def run(e, items):
            for waits, fn, inck, incv in items:
                for k, v in waits:
                    e.wait_ge(semh[k], v)
                if fn is None:
                    continue
                ins = fn(e)
                if inck is not None:
                    ins.then_inc(semh[inck], incv)

        with nc.Block() as block:
            @block.tensor
            def _(e):
                run(e, q['pe'])

            @block.scalar
            def _(e):
                run(e, q['act'])

            @block.vector
            def _(e):
                run(e, q['dve'])

            @block.gpsimd
            def _(e):
                run(e, q['pool'])

            @block.sync
            def _(e):
                run(e, q['sp'])

    def close(self):
        self.es.close()


def smalls_layout(L):
    o = {}
    o['F1'] = 0
    o['MX'] = L * 16
    o['F2'] = 2 * L * 16
    o['FIN'] = 3 * L * 16
    o['QN'] = o['FIN'] + 16
    o['KVN'] = o['QN'] + 4 * L
    o['FB'] = o['KVN'] + 4 * L
    o['N'] = o['FB'] + L
    return o


def build(S, L, debug=False):
    NT = S // T
    NB = S // 128
    nc = bass.Bass("TRN2", target_bir_lowering=False)
    P = Prog(nc)
    SM = smalls_layout(L)

    def din(name, shape, dt=F32):
        return nc.dram_tensor(name, list(shape), dt, kind="ExternalInput").ap()

    def dscr(name, shape, dt):
        kind = "ExternalOutput" if (debug and not name.startswith('w')) else "Internal"
        return nc.dram_tensor(name, list(shape), dt, kind=kind).ap()

    x = din("x", [S, D])
    wg_in = [din("ffn1_w_gate", [L, D, DFF]), din("ffn2_w_gate", [L, D, DFF])]
    wu_in = [din("ffn1_w_up", [L, D, DFF]), din("ffn2_w_up", [L, D, DFF])]
    wd_in = [din("ffn1_w_down", [L, DFF, D]), din("ffn2_w_down", [L, DFF, D])]
    win_in = din("w_in_ext", [L, D, NINX])
    wuq_in = din("w_uq_ext", [L, 512, 2048])
    wukv_in = din("w_ukv_ext", [L, 512, 2048])
    wout_in = din("w_out", [L, D, D])
    smalls_in = din("smalls", [128, SM['N']])
    consts_in = din("consts", [128, 384])
    ropeC_in = din("ropeC", [2, 128, S])
    ropeM_in = din("ropeM", [2, 128, S])
    out = nc.dram_tensor("out", [S, D], F32, kind="ExternalOutput").ap()

    wg_s = [[dscr("wg_s%d_%d" % (l, w), [22, 128, 16, 256], BF16) for w in range(2)] for l in range(L)]
    wu_s = [[dscr("wu_s%d_%d" % (l, w), [22, 128, 16, 256], BF16) for w in range(2)] for l in range(L)]
    wd_s = [[dscr("wd_s%d_%d" % (l, w), [8, 128, 44, 256], BF16) for w in range(2)] for l in range(L)]
    win_s = [dscr("win_s%d" % l, [21, 128, 16, 256], BF16) for l in range(L)]
    wfl_s = [dscr("wfl_s%d" % l, [128, 16, 4], BF16) for l in range(L)]
    wuq_s = [dscr("wuq_s%d" % l, [128, 4, 2048], BF16) for l in range(L)]
    wukv_s = [dscr("wukv_s%d" % l, [128, 4, 2048], BF16) for l in range(L)]
    wout_s = [dscr("wout_s%d" % l, [8, 128, 16, 256], BF16) for l in range(L)]
    xT = dscr("xT", [D, S], F32)
    fmS = dscr("fmS", [NFMS, 128, S], BF16)
    vS = dscr("vS", [S, 2048], BF16)
    flogS = dscr("flogS", [4, S], F32)
    cpS = dscr("cpS", [2, 3, 4, S], BF16)
    mixT = dscr("mixT", [16, 128, S], BF16)

    es = P.es

    uid = [0]

    def sb(name, shape, dt, stack=None):
        uid[0] += 1
        return (stack or es).enter_context(nc.sbuf_tensor("%s_sb%d" % (name, uid[0]), list(shape), dt))

    ps = [es.enter_context(nc.psum_tensor("ps%d" % i, [128, 512], F32)) for i in range(8)]

    cst = sb("cst", [128, 384], F32)
    ident = cst[:, 0:128]
    cbf = sb("cbf", [128, 384], BF16)
    mask2 = cbf[:, 128:384]
    mask4 = sb("mask4", [128, 512], BF16)
    tri_ge = cbf[:, 256:384]
    ones_f = sb("ones_f", [128, 128], F32)
    ones_b = sb("ones_b", [128, 128], BF16)
    epsT = sb("epsT", [128, 1], F32)
    one1 = sb("one1", [128, 1], F32)
    smalls = sb("smalls", [128, SM['N']], F32)
    negb = sb("negb", [4, L], F32)

    P.dma('sp', cst[:], consts_in, writes=['cst'])
    P.dma('sp', smalls[:], smalls_in, writes=['smalls'])
    P.op('dve', lambda e: e.tensor_copy(out=cbf[:], in_=cst[:]), reads=['cst'], writes=['cbf'])
    P.op('dve', lambda e: e.tensor_copy(out=mask4[:, 0:256], in_=cst[:, 128:384]), reads=['cst'], writes=['mask4'])
    P.op('dve', lambda e: e.tensor_copy(out=mask4[:, 256:512], in_=cst[:, 128:384]), reads=['cst'], writes=['mask4'])
    P.op('dve', lambda e: e.memset(ones_f[:], 1.0), writes=['ones_f'])
    P.op('dve', lambda e: e.memset(ones_b[:], 1.0), writes=['ones_b'])
    P.op('dve', lambda e: e.memset(epsT[:], EPS), writes=['epsT'])
    P.op('dve', lambda e: e.memset(one1[:], 1.0), writes=['one1'])
    P.op('dve', lambda e: e.tensor_scalar_mul(out=negb[:], in0=smalls[0:4, SM['FB']:SM['FB'] + L], scalar1=-1.0),
         reads=['smalls'], writes=['negb'])

    def conv_ffn(l, w):
        for g in range(22):
            P.dma('pool', wg_s[l][w][g], wg_in[w][l, :, g * 256:(g + 1) * 256].rearrange("(kc p) c -> p kc c", p=128),
                  writes=[('wg_s', l, w, g)])
            P.dma('pool', wu_s[l][w][g], wu_in[w][l, :, g * 256:(g + 1) * 256].rearrange("(kc p) c -> p kc c", p=128),
                  writes=[('wu_s', l, w, g)])
        for g in range(8):
            for hh in range(2):
                P.dma('pool', wd_s[l][w][g][:, hh * 22:(hh + 1) * 22, :],
                      wd_in[w][l, hh * 2816:(hh + 1) * 2816, g * 256:(g + 1) * 256].rearrange("(kc p) c -> p kc c", p=128),
                      writes=[('wd_s', l, w, g, hh)])

    def conv_mix(l):
        for g in range(21):
            P.dma('pool', win_s[l][g], win_in[l, :, g * 256:(g + 1) * 256].rearrange("(kc p) c -> p kc c", p=128),
                  writes=[('win_s', l, g)])
        P.dma('pool', wfl_s[l], win_in[l, :, 5376:5380].rearrange("(kc p) c -> p kc c", p=128), writes=[('wfl_s', l)])
        P.dma('pool', wuq_s[l], wuq_in[l].rearrange("(kc p) c -> p kc c", p=128), writes=[('wuq_s', l)])
        P.dma('pool', wukv_s[l], wukv_in[l].rearrange("(kc p) c -> p kc c", p=128), writes=[('wukv_s', l)])
        for g in range(8):
            P.dma('pool', wout_s[l][g], wout_in[l, :, g * 256:(g + 1) * 256].rearrange("(kc p) c -> p kc c", p=128),
                  writes=[('wout_s', l, g)])

    conv_ffn(0, 0)
    conv_mix(0)

    def mm(out_ap, lhsT, rhs, start, stop, reads, writes, tick):
        P.op('pe', lambda e: e.matmul(out_ap, lhsT=lhsT, rhs=rhs, start=start, stop=stop), reads, writes, tick)

    cpy_rr = [0]

    def copy_any(out_ap, in_ap, reads, writes, eng=None):
        if eng is None:
            eng = 'act' if (cpy_rr[0] % 2 == 0) else 'dve'
            cpy_rr[0] += 1
        if eng == 'act':
            P.op('act', lambda e: e.copy(out=out_ap, in_=in_ap), reads, writes)
        else:
            P.op('dve', lambda e: e.tensor_copy(out=out_ap, in_=in_ap), reads, writes)

    def rms_p1(src_fn, nch, rkey_fn, sq, acc, akey):
        for c in range(nch):
            s = c % 2
            src = src_fn(c)
            if c == 0:
                P.op('act', (lambda e, src=src: e.activation(out=acc[:], in_=src, func=AF.Square)),
                     reads=[rkey_fn(c)], writes=[akey])
            else:
                P.op('act', (lambda e, s=s, src=src: e.activation(out=sq[:, s, :], in_=src, func=AF.Square)),
                     reads=[rkey_fn(c)], writes=[('sq', s)])
                P.op('dve', (lambda e, s=s: e.tensor_tensor(out=acc[:], in0=acc[:], in1=sq[:, s, :], op=ALU.add)),
                     reads=[('sq', s), akey], writes=[akey])

    def rms_p2(src_fn, nch, gcol_fn, dst_fn, dim, rkey_fn, wkey_fn, rstd, bank, rkey, acc, akey):
        mm(ps[bank][:], ones_f[:], acc[:], True, True, reads=[akey], writes=[('ps', bank)], tick=True)
        P.op('act', lambda e: e.activation(out=rstd[:], in_=ps[bank][:], func=AF.Sqrt, bias=epsT[:], scale=1.0 / dim),
             reads=[('ps', bank)], writes=[rkey])
        P.op('dve', lambda e: e.reciprocal(out=rstd[:], in_=rstd[:]), reads=[rkey], writes=[rkey])
        for c in range(nch):
            src = src_fn(c)
            dst = dst_fn(c)
            g = gcol_fn(c)
            P.op('dve', (lambda e, src=src, dst=dst, g=g: e.scalar_tensor_tensor(
                out=dst, in0=src, scalar=g, in1=rstd[:], op0=ALU.mult, op1=ALU.mult)),
                reads=[rkey_fn(c), rkey], writes=[wkey_fn(c)])

    def rmsnorm(src_fn, nch, gcol_fn, dst_fn, dim, rkey_fn, wkey_fn, sq, rstd, bank, rkey, acc=None):
        rms_p1(src_fn, nch, rkey_fn, sq, acc, 'acc')
        rms_p2(src_fn, nch, gcol_fn, dst_fn, dim, rkey_fn, wkey_fn, rstd, bank, rkey, acc, 'acc')

    def load_xt(xt, tt):
        P.dma('sp', xt[:], xT[:, tt * T:(tt + 1) * T].rearrange("(c p) t -> p c t", p=128),
              reads=[('xT', tt)], writes=[('xt', c) for c in range(KC)])

    def store_xt(xt, tt):
        P.dma('sp', xT[:, tt * T:(tt + 1) * T].rearrange("(c p) t -> p c t", p=128), xt[:],
              reads=[('xt', c) for c in range(KC)], writes=[('xT', tt)])

    def phase_T0():
        with ExitStack() as ph:
            xin = sb("xin", [128, 4, D], F32, ph)
            stg = sb("stg0", [128, KC, T], F32, ph)
            for tt in range(NT):
                P.dma('sp', xin[:], x[tt * T:(tt + 1) * T, :].rearrange("(b p) d -> p b d", p=128), writes=['xin'])
                for c in range(KC):
                    bank = c % 4
                    for b in range(4):
                        o_ap = ps[bank][:, b * 128:(b + 1) * 128]
                        i_ap = xin[:, b, c * 128:(c + 1) * 128]
                        P.op('pe', (lambda e, o_ap=o_ap, i_ap=i_ap: e.transpose(out=o_ap, in_=i_ap, identity=ident)),
                             reads=['xin', 'cst'], writes=[('ps', bank)], tick=(b == 3))
                    copy_any(stg[:, c, :], ps[bank][:], [('ps', bank)], [('stg', c)])
                P.dma('sp', xT[:, tt * T:(tt + 1) * T].rearrange("(c p) t -> p c t", p=128), stg[:],
                      reads=[('stg', c) for c in range(KC)], writes=[('xT', tt)])
            P.barrier()

    def phase_ffn(l, w):
        gbase = (SM['F1'] if w == 0 else SM['F2']) + l * 16
        with ExitStack() as ph:
            xt = sb("xt", [128, KC, T], F32, ph)
            hT = sb("hT", [128, 2, KC, T], BF16, ph)
            actT = sb("actT", [128, FC, T], BF16, ph)
            wgb = sb("wgb", [128, 2, 16, 256], BF16, ph)
            wub = sb("wub", [128, 2, 16, 256], BF16, ph)
            wdb = sb("wdb", [128, 2, 44, 256], BF16, ph)
            xr = sb("xr", [128, 2, T], F32, ph)
            acc = sb("acc", [128, T], F32, ph)
            sg = sb("sg", [128, 2, T], F32, ph)
            rstd = sb("rstd", [128, 2, T], F32, ph)
            ngu = [0]
            nd = [0]

            def ensure_gu(q):
                q = min(q, NT * 22 - 1)
                while ngu[0] <= q:
                    g_ = ngu[0] % 22
                    s_ = ngu[0] % 2
                    P.dma('sp', wgb[:, s_], wg_s[l][w][g_], reads=[('wg_s', l, w, g_)], writes=[('wgb', s_)])
                    P.dma('sp', wub[:, s_], wu_s[l][w][g_], reads=[('wu_s', l, w, g_)], writes=[('wub', s_)])
                    ngu[0] += 1

            def ensure_d(q):
                q = min(q, NT * 8 - 1)
                while nd[0] <= q:
                    g_ = nd[0] % 8
                    s_ = nd[0] % 2
                    for hh in range(2):
                        P.dma('sp', wdb[:, s_, hh * 22:(hh + 1) * 22, :], wd_s[l][w][g_][:, hh * 22:(hh + 1) * 22, :],
                              reads=[('wd_s', l, w, g_, hh)], writes=[('wdb', s_, hh)])
                    nd[0] += 1

            def ssq_load(tt, c):
                s_ = c % 2
                P.dma('sp', xr[:, s_, :], xT[c * 128:(c + 1) * 128, tt * T:(tt + 1) * T], writes=[('xr', s_)])

            def ssq_chunk(tt, c):
                s_ = c % 2
                if c == 0:
                    P.op('act', (lambda e: e.activation(out=acc[:], in_=xr[:, s_, :], func=AF.Square)),
                         reads=[('xr', s_)], writes=['acc'])
                else:
                    P.op('act', (lambda e: e.activation(out=xr[:, s_, :], in_=xr[:, s_, :], func=AF.Square)),
                         reads=[('xr', s_)], writes=[('xr', s_)])
                    P.op('dve', (lambda e: e.tensor_tensor(out=acc[:], in0=acc[:], in1=xr[:, s_, :], op=ALU.add)),
                         reads=[('xr', s_), 'acc'], writes=['acc'])

            def ssq_stream(tt):
                ssq_load(tt, 0)
                for c in range(KC):
                    if c + 1 < KC:
                        ssq_load(tt, c + 1)
                    ssq_chunk(tt, c)

            def ssq_finish(par):
                mm(ps[6][:], ones_f[:], acc[:], True, True, reads=['acc'], writes=[('ps', 6)], tick=True)
                P.op('act', lambda e: e.activation(out=rstd[:, par, :], in_=ps[6][:], func=AF.Sqrt, bias=epsT[:], scale=1.0 / D),
                     reads=[('ps', 6)], writes=[('rstd', par)])
                P.op('dve', lambda e: e.reciprocal(out=rstd[:, par, :], in_=rstd[:, par, :]),
                     reads=[('rstd', par)], writes=[('rstd', par)])

            def load_chunk(tt, c):
                P.dma('sp', xt[:, c, :], xT[c * 128:(c + 1) * 128, tt * T:(tt + 1) * T], reads=[('xT', tt, c)],
                      writes=[('xt', c)])

            def make_h(par, c):
                P.op('dve', (lambda e: e.scalar_tensor_tensor(
                    out=hT[:, par, c, :], in0=xt[:, c, :], scalar=smalls[:, gbase + c:gbase + c + 1], in1=rstd[:, par, :],
                    op0=ALU.mult, op1=ALU.mult)),
                    reads=[('xt', c), ('rstd', par)], writes=[('hT', par, c)])

            ensure_gu(0)
            for c in range(KC):
                load_chunk(0, c)
            for c in range(KC):
                s_ = c % 2
                if c == 0:
                    P.op('act', (lambda e: e.activation(out=acc[:], in_=xt[:, 0, :], func=AF.Square)),
                         reads=[('xt', 0)], writes=['acc'])
                else:
                    P.op('act', (lambda e, c=c, s_=s_: e.activation(out=xr[:, s_, :], in_=xt[:, c, :], func=AF.Square)),
                         reads=[('xt', c)], writes=[('xr', s_)])
                    P.op('dve', (lambda e, s_=s_: e.tensor_tensor(out=acc[:], in0=acc[:], in1=xr[:, s_, :], op=ALU.add)),
                         reads=[('xr', s_), 'acc'], writes=['acc'])
            ssq_finish(0)
            for c in range(KC):
                make_h(0, c)
            for tt in range(NT):
                par = tt % 2
                more = tt + 1 < NT
                for g in range(22):
                    q = tt * 22 + g
                    ensure_gu(q + 1)
                    s = q % 2
                    if g == 1 and more:
                        ssq_load(tt + 1, 0)
                    if g == 12 and more:
                        ssq_finish(1 - par)
                    if g == 20:
                        ensure_d(tt * 8)
                    for j in range(2):
                        f = 2 * g + j
                        bG = f % 2
                        bU = 2 + f % 2
                        ci = f - 4
                        if more and 0 <= ci < KC:
                            if ci + 1 < KC:
                                ssq_load(tt + 1, ci + 1)
                            ssq_chunk(tt + 1, ci)
                        for k in range(KC):
                            mm(ps[bG][:], wgb[:, s, k, j * 128:(j + 1) * 128], hT[:, par, k, :], k == 0, k == KC - 1,
                               reads=[('wgb', s), ('hT', par, k)], writes=[('ps', bG)], tick=(k == KC - 1))
                        for k in range(KC):
                            mm(ps[bU][:], wub[:, s, k, j * 128:(j + 1) * 128], hT[:, par, k, :], k == 0, k == KC - 1,
                               reads=[('wub', s)], writes=[('ps', bU)], tick=(k == KC - 1))
                        P.op('act', (lambda e, f=f, bG=bG: e.activation(out=sg[:, f % 2, :], in_=ps[bG][:], func=AF.Silu)),
                             reads=[('ps', bG)], writes=[('sg', f % 2)])
                        P.op('dve', (lambda e, f=f, bU=bU: e.tensor_tensor(out=actT[:, f, :], in0=ps[bU][:], in1=sg[:, f % 2, :],
                                                                           op=ALU.mult)),
                             reads=[('ps', bU), ('sg', f % 2)], writes=[('actT', f)])
                akeys = [('actT', f) for f in range(FC)]
                for g in range(8):
                    q = tt * 8 + g
                    ensure_d(q + 1)
                    s = q % 2
                    if g == 6 and more:
                        ensure_gu((tt + 1) * 22)
                    for j in range(2):
                        o = 2 * g + j
                        bY = 4 + o % 2
                        for k in range(FC):
                            mm(ps[bY][:], wdb[:, s, k, j * 128:(j + 1) * 128], actT[:, k, :], k == 0, k == FC - 1,
                               reads=[('wdb', s, 0), ('wdb', s, 1)] + (akeys if k == 0 else []), writes=[('ps', bY)],
                               tick=(k == FC - 1))
                        P.op('dve', (lambda e, o=o, bY=bY: e.scalar_tensor_tensor(
                            out=xt[:, o, :], in0=ps[bY][:], scalar=0.5, in1=xt[:, o, :], op0=ALU.mult, op1=ALU.add)),
                            reads=[('ps', bY), ('xt', o)], writes=[('xt', o)])
                        P.dma('sp', xT[o * 128:(o + 1) * 128, tt * T:(tt + 1) * T], xt[:, o, :], reads=[('xt', o)],
                              writes=[('xT', tt, o)])
                        if more:
                            load_chunk(tt + 1, o)
                            if o > 0:
                                make_h(1 - par, o - 1)
                if more:
                    make_h(1 - par, KC - 1)
            P.barrier()

    def phase_m1(l):
        gbase = SM['MX'] + l * 16
        with ExitStack() as ph:
            xt = sb("xt1", [128, KC, T], F32, ph)
            hT2 = sb("hT1", [128, 2, KC, T], BF16, ph)
            wb = sb("wb1", [128, 3, 16, 256], BF16, ph)
            wfl = sb("wfl", [128, 16, 4], BF16, ph)
            wuq = sb("wuq", [128, 4, 2048], BF16, ph)
            wukv = sb("wukv", [128, 4, 2048], BF16, ph)
            sq = sb("sq1", [128, 2, T], F32, ph)
            acc = sb("acc1", [128, T], F32, ph)
            acc2 = sb("acc2", [128, T], F32, ph)
            acc3 = sb("acc3", [128, T], F32, ph)
            rstd3 = sb("rstd3", [128, T], F32, ph)
            rstd = sb("rstd1", [128, T], F32, ph)
            rstd2 = sb("rstd2", [128, T], F32, ph)
            cq = sb("cq", [128, 4, T], F32, ph)
            ckv = sb("ckv", [128, 4, T], F32, ph)
            cqn = sb("cqn", [128, 4, T], BF16, ph)
            ckvn = sb("ckvn", [128, 4, T], BF16, ph)
            rC = sb("rC", [128, 2, T], F32, ph)
            rM = sb("rM", [128, 2, T], F32, ph)
            tA = sb("tA", [128, T], F32, ph)
            tB = sb("tB", [128, T], F32, ph)
            stg = sb("stg1", [128, 4, T], BF16, ph)
            vst = sb("vst", [128, 4, 2048], BF16, ph)
            flg = sb("flg", [4, T], F32, ph)
            P.dma('sp', wfl[:], wfl_s[l], reads=[('wfl_s', l)], writes=['wfl'])
            P.dma('sp', wuq[:], wuq_s[l], reads=[('wuq_s', l)], writes=['wuq'])
            P.dma('sp', wukv[:], wukv_s[l], reads=[('wukv_s', l)], writes=['wukv'])
            bank_rr = [0]
            stg_rr = [0]
            nxt = [0]

            def ensure_loaded(q):
                q = min(q, NT * 21 - 1)
                while nxt[0] <= q:
                    g_ = nxt[0] % 21
                    P.dma('sp', wb[:, g_ % 3], win_s[l][g_], reads=[('win_s', l, g_)], writes=[('wb', g_ % 3)])
                    nxt[0] += 1

            def nbank():
                b = bank_rr[0] % 6
                bank_rr[0] += 1
                return b

            def store_fm(src_ps_bank, dst_chunk, tt, pre=None):
                s = stg_rr[0] % 4
                stg_rr[0] += 1
                if pre is None:
                    copy_any(stg[:, s, :], ps[src_ps_bank][:], [('ps', src_ps_bank)], [('stg', s)])
                else:
                    pre(stg[:, s, :], ('stg', s))
                P.dma('sp', fmS[dst_chunk][:, tt * T:(tt + 1) * T], stg[:, s, :], reads=[('stg', s)],
                      writes=[('fmS', dst_chunk, tt)])

            def rope_pair(bA, bB, tab, tabkey, dst_chunk, tt):
                P.op('dve', lambda e: e.tensor_tensor(out=tA[:], in0=ps[bA][:], in1=tab[:, 0, :], op=ALU.mult),
                     reads=[('ps', bA), tabkey], writes=['tA'])
                P.op('dve', lambda e: e.tensor_tensor(out=tB[:], in0=ps[bB][:], in1=tab[:, 1, :], op=ALU.mult),
                     reads=[('ps', bB), tabkey], writes=['tB'])

                def pre(dst, dkey):
                    P.op('dve', lambda e: e.tensor_tensor(out=dst, in0=tA[:], in1=tB[:], op=ALU.add),
                         reads=['tA', 'tB'], writes=[dkey])
                store_fm(None, dst_chunk, tt, pre=pre)

            load_xt(xt, 0)
            ensure_loaded(1)

            def xnorm_p1():
                rms_p1(lambda c: xt[:, c, :], KC, lambda c: ('xt', c), sq, acc, 'acc')

            def xnorm_p2(par):
                rms_p2(lambda c: xt[:, c, :], KC, lambda c: smalls[:, gbase + c:gbase + c + 1],
                       lambda c: hT2[:, par, c, :], D, lambda c: ('xt', c), lambda c: ('hT', par, c), rstd, 6, 'rstd',
                       acc, 'acc')

            xnorm_p1()
            xnorm_p2(0)
            if NT > 1:
                load_xt(xt, 1)
            qn_base = SM['QN'] + l * 4
            kvn_base = SM['KVN'] + l * 4
            for tt in range(NT):
                par = tt % 2
                hT = hT2[:, par]
                P.dma('sp', rC[:], ropeC_in[:, :, tt * T:(tt + 1) * T].rearrange("a p t -> p a t"), writes=['rC'])
                P.dma('sp', rM[:], ropeM_in[:, :, tt * T:(tt + 1) * T].rearrange("a p t -> p a t"), writes=['rM'])
                hkeys = [('hT', par, c) for c in range(KC)]
                pend = None
                for g in range(17):
                    s = g % 3
                    ensure_loaded(tt * 21 + g + 2)
                    banks = []
                    for j in range(2):
                        b = nbank()
                        banks.append(b)
                        for k in range(KC):
                            mm(ps[b][:], wb[:, s, k, j * 128:(j + 1) * 128], hT[:, k, :], k == 0, k == KC - 1,
                               reads=[('wb', s)] + (hkeys if k == 0 else []), writes=[('ps', b)], tick=(k == KC - 1))
                    for j in range(2):
                        cidx = 2 * g + j
                        b = banks[j]
                        if cidx < 8:
                            store_fm(b, cidx, tt)
                        elif cidx < 12:
                            copy_any(cq[:, cidx - 8, :], ps[b][:], [('ps', b)], [('cq', cidx - 8)])
                        elif cidx < 16:
                            copy_any(ckv[:, cidx - 12, :], ps[b][:], [('ps', b)], [('ckv', cidx - 12)])
                    if g == 6:
                        rms_p1(lambda c: cq[:, c, :], 4, lambda c: ('cq', c), sq, acc2, 'acc2')
                    if g == 8:
                        rms_p2(lambda c: cq[:, c, :], 4, lambda c: smalls[:, qn_base + c:qn_base + c + 1],
                               lambda c: cqn[:, c, :], 512, lambda c: ('cq', c), lambda c: ('cqn', c), rstd2, 7, 'rstd2',
                               acc2, 'acc2')
                        rms_p1(lambda c: ckv[:, c, :], 4, lambda c: ('ckv', c), sq, acc3, 'acc3')
                    if g == 10:
                        rms_p2(lambda c: ckv[:, c, :], 4, lambda c: smalls[:, kvn_base + c:kvn_base + c + 1],
                               lambda c: ckvn[:, c, :], 512, lambda c: ('ckv', c), lambda c: ('ckvn', c), rstd3, 7, 'rstd3',
                               acc3, 'acc3')
                    if g == 12 and tt + 1 < NT:
                        xnorm_p1()
                    if g == 8:
                        rope_pair(banks[0], banks[1], rM, 'rM', 28, tt)
                    elif 9 <= g <= 12:
                        rope_pair(banks[0], banks[1], rC, 'rC', 29 + (g - 9), tt)
                    elif g >= 13:
                        rope_pair(banks[0], banks[1], rC, 'rC', 33 + (g - 13), tt)
                b = nbank()
                for k in range(KC):
                    mm(ps[b][0:4, :], wfl[:, k, :], hT[:, k, :], k == 0, k == KC - 1,
                       reads=['wfl'] + (hkeys if k == 0 else []), writes=[('ps', b)], tick=(k == KC - 1))
                copy_any(flg[:], ps[b][0:4, :], [('ps', b)], ['flg'])
                P.dma('sp', flogS[:, tt * T:(tt + 1) * T], flg[:], reads=['flg'], writes=[('flogS', tt)])
                for vi, (g0, col0) in enumerate(((17, 0), (19, 1536))):
                    ensure_loaded(tt * 21 + g0 + 2)
                    for blk in range(4):
                        b = nbank()
                        for hf in range(2):
                            s = (g0 + hf) % 3
                            for k in range(KC):
                                mm(ps[b][:, hf * 256:(hf + 1) * 256], hT[:, k, blk * 128:(blk + 1) * 128], wb[:, s, k, :],
                                   k == 0, k == KC - 1, reads=[('wb', s)] + (hkeys if k == 0 else []),
                                   writes=[('ps', b)], tick=(k == KC - 1))
                        copy_any(vst[:, blk, col0:col0 + 512], ps[b][:], [('ps', b)], [('vst', blk, vi)])
                if tt + 1 < NT:
                    xnorm_p2(1 - par)
                    if tt + 2 < NT:
                        load_xt(xt, tt + 2)
                qkeys = [('cqn', c) for c in range(4)]
                for h in range(8):
                    b = nbank()
                    for k in range(4):
                        mm(ps[b][:], wuq[:, k, h * 128:(h + 1) * 128], cqn[:, k, :], k == 0, k == 3,
                           reads=['wuq'] + (qkeys if k == 0 else []), writes=[('ps', b)], tick=(k == 3))
                    store_fm(b, 8 + h, tt)
                for i in range(4):
                    bA = nbank()
                    bB = nbank()
                    for k in range(4):
                        mm(ps[bA][:], wuq[:, k, 1024 + i * 128:1024 + (i + 1) * 128], cqn[:, k, :], k == 0, k == 3,
                           reads=['wuq'] + (qkeys if k == 0 else []), writes=[('ps', bA)], tick=(k == 3))
                    for k in range(4):
                        mm(ps[bB][:], wuq[:, k, 1536 + i * 128:1536 + (i + 1) * 128], cqn[:, k, :], k == 0, k == 3,
                           reads=['wuq'], writes=[('ps', bB)], tick=(k == 3))
                    rope_pair(bA, bB, rM, 'rM', 24 + i, tt)
                kkeys = [('ckvn', c) for c in range(4)]
                for h in range(8):
                    b = nbank()
                    for k in range(4):
                        mm(ps[b][:], wukv[:, k, h * 128:(h + 1) * 128], ckvn[:, k, :], k == 0, k == 3,
                           reads=['wukv'] + (kkeys if k == 0 else []), writes=[('ps', b)], tick=(k == 3))
                    store_fm(b, 16 + h, tt)
                for blk in range(4):
                    for hf in range(2):
                        b = nbank()
                        for k in range(4):
                            mm(ps[b][:], ckvn[:, k, blk * 128:(blk + 1) * 128], wukv[:, k, 1024 + hf * 512:1024 + (hf + 1) * 512],
                               k == 0, k == 3, reads=['wukv'] + (kkeys if k == 0 else []), writes=[('ps', b)], tick=(k == 3))
                        copy_any(vst[:, blk, 512 + hf * 512:512 + (hf + 1) * 512], ps[b][:], [('ps', b)],
                                 [('vst', blk, 2 + hf)])
                P.dma('sp', vS[tt * T:(tt + 1) * T, :].rearrange("(b p) c -> p b c", p=128), vst[:],
                      reads=[('vst', blk, i) for blk in range(4) for i in range(4)], writes=[('vS', tt)])
            P.barrier()

    def phase_m2(l):
        with ExitStack() as ph:
            fl = sb("fl", [4, S], F32, ph)
            z = sb("z", [4, S], F32, ph)
            cs = sb("cs", [4, S], F32, ph)
            on4 = sb("on4", [4, S], F32, ph)
            pb = sb("pb", [4, 3, S], BF16, ph)
            nb_ = sb("nb_", [4, 3, S], BF16, ph)
            pf = sb("pf", [4, S], F32, ph)
            P.dma('sp', fl[:], flogS, writes=['fl'])
            P.op('dve', lambda e: e.memset(on4[:], 1.0), writes=['on4'])
            P.op('act', lambda e: e.activation(out=z[:], in_=fl[:], func=AF.Exp, bias=negb[:, l:l + 1], scale=-1.0),
                 reads=['fl'], writes=['z'])
            P.op('act', lambda e: e.activation(out=z[:], in_=z[:], func=AF.Ln, bias=one1[0:4, :], scale=1.0),
                 reads=['z'], writes=['z'])
            P.op('dve', lambda e: e.tensor_tensor_scan(out=cs[:], data0=on4[:], data1=z[:], initial=0.0,
                                                       op0=ALU.mult, op1=ALU.add), reads=['on4', 'z'], writes=['cs'])
            P.op('dve', lambda e: e.tensor_scalar_mul(out=cs[:], in0=cs[:], scalar1=float(np.sqrt(128.0))),
                 reads=['cs'], writes=['cs'])
            for i in range(3):
                P.op('dve', (lambda e, i=i: e.tensor_copy(out=pb[:, i, :], in_=cs[:])), reads=['cs'], writes=[('pb', i)])
                P.op('dve', (lambda e, i=i: e.tensor_scalar_mul(out=nb_[:, i, :], in0=pb[:, i, :], scalar1=-1.0)),
                     reads=[('pb', i)], writes=[('nb', i)])
                if i < 2:
                    P.op('dve', (lambda e, i=i: e.tensor_copy(out=pf[:], in_=pb[:, i, :])), reads=[('pb', i)], writes=['pf'])
                    P.op('dve', lambda e: e.tensor_tensor(out=cs[:], in0=cs[:], in1=pf[:], op=ALU.subtract),
                         reads=['cs', 'pf'], writes=['cs'])
            P.dma('sp', cpS[0].rearrange("i h s -> h i s"), pb[:], reads=[('pb', i) for i in range(3)], writes=['cpS0'])
            P.dma('sp', cpS[1].rearrange("i h s -> h i s"), nb_[:], reads=[('nb', i) for i in range(3)], writes=['cpS1'])
            P.barrier()

    def phase_m3(l):
        with ExitStack() as ph:
            qT = sb("qT", [128, 2, S], BF16, ph)
            kT = sb("kT", [128, 2, S], BF16, ph)
            qx = sb("qx", [128, 2, S], BF16, ph)
            kx = sb("kx", [128, 2, S], BF16, ph)
            krT = sb("krT", [128, S], BF16, ph)
            vt = sb("vt", [128, 2, NB, 128], BF16, ph)
            pT = sb("pT", [128, 3, T], BF16, ph)
            rden = sb("rden", [128, T], F32, ph)
            ostg = sb("ostg", [128, 2, T], BF16, ph)
            o_rr = [0]
            hslot = [0]
            P.op('dve', lambda e: e.memset(qx[:, :, :], 0.0), writes=[('qx', 0), ('qx', 1)])
            P.op('dve', lambda e: e.memset(kx[:, :, :], 0.0), writes=[('kx', 0), ('kx', 1)])
            P.op('dve', lambda e: e.memset(krT[:, :], 0.0), writes=['krT'])
            P.op('dve', lambda e: e.memset(qx[0:6, :, :], 1.0), writes=[('qx', 0), ('qx', 1)])
            P.op('dve', lambda e: e.memset(kx[0:6, :, :], 1.0), writes=[('kx', 0), ('kx', 1)])

            def finalize(j, chunk):
                bO = 3 + j % 2
                bD = 5 + j % 2
                s = o_rr[0] % 2
                o_rr[0] += 1
                P.op('dve', lambda e: e.reciprocal(out=rden[:], in_=ps[bD][:]), reads=[('ps', bD)], writes=['rden'])
                P.op('dve', lambda e: e.tensor_tensor(out=ostg[:, s, :], in0=ps[bO][:], in1=rden[:], op=ALU.mult),
                     reads=[('ps', bO), 'rden'], writes=[('ostg', s)])
                P.dma('sp', mixT[chunk][:, j * T:(j + 1) * T], ostg[:, s, :], reads=[('ostg', s)],
                      writes=[('mixT', chunk, j)])

            def causal_head(hs, scale, extra, chunk):
                tiles = [(j, kb) for j in range(NT) for kb in range(4 * j + 4)]
                n = len(tiles)
                rk = [('qT', hs), ('kT', hs), ('qx', hs), ('kx', hs), 'krT']

                def geom(i):
                    j, kb = tiles[i]
                    a = kb - 4 * j
                    c0 = 128 * a if a > 0 else 0
                    return j, kb, a, c0

                def emitQK(i):
                    j, kb, a, c0 = geom(i)
                    s = i % 3
                    mm(ps[s][:, c0:T], kT[:, hs, kb * 128:(kb + 1) * 128], qT[:, hs, j * T + c0:(j + 1) * T], True,
                       extra is None, reads=rk, writes=[('ps', s)], tick=(extra is None))
                    if extra is not None:
                        kr_, lt, rt = extra
                        mm(ps[s][:, c0:T], lt[0:kr_, kb * 128:(kb + 1) * 128], rt[0:kr_, j * T + c0:(j + 1) * T], False, True,
                           reads=rk, writes=[('ps', s)], tick=True)
                    P.op('act', (lambda e: e.activation(out=pT[:, s, c0:T], in_=ps[s][:, c0:T], func=AF.Exp, scale=scale)),
                         reads=[('ps', s)], writes=[('pT', s)])
                    if a >= 0:
                        P.op('dve', (lambda e: e.tensor_tensor(out=pT[:, s, c0:c0 + 128], in0=pT[:, s, c0:c0 + 128],
                                                               in1=tri_ge, op=ALU.mult)),
                             reads=[('pT', s)], writes=[('pT', s)])

                def emitPV(i):
                    j, kb, a, c0 = geom(i)
                    s = i % 3
                    first = (kb == 0)
                    last = (kb == 4 * j + 3)
                    bO = 3 + j % 2
                    bD = 5 + j % 2
                    mm(ps[bO][:, c0:T], vt[:, hs, kb, :], pT[:, s, c0:T], first, last,
                       reads=[('pT', s), ('vt', hs)], writes=[('ps', bO)], tick=False)
                    mm(ps[bD][:, c0:T], ones_b[:], pT[:, s, c0:T], first, last,
                       reads=[('pT', s)], writes=[('ps', bD)], tick=True)
                    if last:
                        finalize(j, chunk)

                for i in range(n + 2):
                    if i < n:
                        emitQK(i)
                    if i >= 2:
                        emitPV(i - 2)

            P.dma('sp', krT[0:64, :], fmS[28][0:64, :], reads=[('fmS', 28, tt) for tt in range(NT)], writes=['krT'])

            def prefetch(idx):
                hs = idx % 2
                if idx < 4:
                    h = idx
                    P.dma('sp', qT[:, hs, :], fmS[h], reads=[('fmS', h, tt) for tt in range(NT)], writes=[('qT', hs)])
                    P.dma('sp', kT[:, hs, :], fmS[4 + h], reads=[('fmS', 4 + h, tt) for tt in range(NT)], writes=[('kT', hs)])
                    P.dma('sp', kx[0:3, hs, :], cpS[0, :, h, :], reads=['cpS0'], writes=[('kx', hs)])
                    P.dma('sp', qx[3:6, hs, :], cpS[1, :, h, :], reads=['cpS1'], writes=[('qx', hs)])
                    P.dma('sp', vt[:, hs], vS[:, h * 128:(h + 1) * 128].rearrange("(b p) c -> p b c", p=128),
                          reads=[('vS', tt) for tt in range(NT)], writes=[('vt', hs)])
                else:
                    h = idx - 4
                    P.dma('sp', qT[:, hs, :], fmS[8 + h], reads=[('fmS', 8 + h, tt) for tt in range(NT)], writes=[('qT', hs)])
                    P.dma('sp', kT[:, hs, :], fmS[16 + h], reads=[('fmS', 16 + h, tt) for tt in range(NT)], writes=[('kT', hs)])
                    P.dma('sp', qx[0:64, hs, :], fmS[24 + h // 2][(h % 2) * 64:(h % 2) * 64 + 64, :],
                          reads=[('fmS', 24 + h // 2, tt) for tt in range(NT)], writes=[('qx', hs)])
                    P.dma('sp', vt[:, hs], vS[:, 512 + h * 128:512 + (h + 1) * 128].rearrange("(b p) c -> p b c", p=128),
                          reads=[('vS', tt) for tt in range(NT)], writes=[('vt', hs)])

            prefetch(0)
            for idx in range(12):
                hs = idx % 2
                if idx + 1 < 12:
                    prefetch(idx + 1)
                if idx < 4:
                    causal_head(hs, float(128.0 ** -0.5), (128, kx[:, hs, :], qx[:, hs, :]), idx)
                else:
                    causal_head(hs, float(192.0 ** -0.5), (128, krT[:, :], qx[:, hs, :]), idx)
            P.barrier()

    def phase_m3c(l):
        with ExitStack() as ph:
            qT2 = sb("cqT", [128, 2, S], BF16, ph)
            kT2 = sb("ckT", [128, 2, S], BF16, ph)
            qdb = {4: sb("qd4", [128, 2, S], BF16, ph), 16: sb("qd16", [128, 2, S], BF16, ph)}
            kdb = {4: sb("kd4", [128, 2, S], BF16, ph), 16: sb("kd16", [128, 2, S], BF16, ph)}
            vdb = {1: sb("vd1", [128, 2, S], BF16, ph), 4: sb("vd4", [128, 2, S], BF16, ph),
                   16: sb("vd16", [128, 2, S], BF16, ph)}
            num = sb("num", [128, S], F32, ph)
            den = sb("den", [128, S], F32, ph)
            pT = sb("cpT", [128, 3, 512], BF16, ph)
            rden = sb("crden", [128, T], F32, ph)
            ostg = sb("costg", [128, 2, T], BF16, ph)
            scale = float(128.0 ** -0.5)
            it = [0]
            grp = [0]
            DIL = (1, 4, 16)

            def prep_loads(h):
                hp = h % 2
                col = 1536 + h * 128
                P.dma('sp', qT2[:, hp, :], fmS[29 + h], reads=[('fmS', 29 + h, tt) for tt in range(NT)], writes=[('qT', hp)])
                P.dma('sp', kT2[:, hp, :], fmS[33 + h], reads=[('fmS', 33 + h, tt) for tt in range(NT)], writes=[('kT', hp)])
                for d in DIL:
                    for r in range(d):
                        P.dma('sp', vdb[d][:, hp, :].rearrange("p (r kb c) -> p r kb c", r=d, c=128)[:, r],
                              vS[:, col:col + 128].rearrange("(kb p r) c -> r p kb c", p=128, r=d)[r],
                              reads=[('vS', tt) for tt in range(NT)], writes=[('vd', d, hp)])

            def prep_copies(h):
                hp = h % 2
                for d in (4, 16):
                    P.op('dve', (lambda e, d=d: e.tensor_copy(out=qdb[d][:, hp, :].rearrange("p (r m) -> p r m", r=d),
                                                              in_=qT2[:, hp, :].rearrange("p (m r) -> p r m", r=d))),
                         reads=[('qT', hp)], writes=[('qd', d, hp)])
                    if d == 4:
                        P.op('dve', (lambda e, d=d: e.tensor_copy(out=kdb[d][:, hp, :].rearrange("p (r m) -> p r m", r=d),
                                                                  in_=kT2[:, hp, :].rearrange("p (m r) -> p r m", r=d))),
                             reads=[('kT', hp)], writes=[('kd', d, hp)])
                    else:
                        P.op('act', (lambda e, d=d: e.copy(out=kdb[d][:, hp, :].rearrange("p (r m) -> p r m", r=d),
                                                           in_=kT2[:, hp, :].rearrange("p (m r) -> p r m", r=d))),
                             reads=[('kT', hp)], writes=[('kd', d, hp)])

            def qk_stage(st):
                (bi, d, Lr, qsrc, ksrc, vview, qk_keys, vkey, r, n0, ng, nn, nb2, g_i, i) = st
                s = i % 3
                base = r * Lr
                c_lo = 0 if (n0 + nn) > 0 else 128
                c_hi = 256 * nb2
                for u in range(nb2):
                    n_ = n0 + nn + u
                    q_ap = qsrc[:, base + n_ * 128: base + (n_ + 1) * 128]
                    lastu = (u == nb2 - 1)
                    if n_ > 0:
                        mm(ps[s][:, u * 256:u * 256 + 128], ksrc[:, base + (n_ - 1) * 128: base + n_ * 128], q_ap, True, True,
                           reads=qk_keys, writes=[('ps', s)], tick=False)
                    mm(ps[s][:, u * 256 + 128:u * 256 + 256], ksrc[:, base + n_ * 128: base + (n_ + 1) * 128], q_ap, True, True,
                       reads=qk_keys, writes=[('ps', s)], tick=lastu)
                P.op('act', (lambda e: e.activation(out=pT[:, s, c_lo:c_hi], in_=ps[s][:, c_lo:c_hi], func=AF.Exp, scale=scale)),
                     reads=[('ps', s)], writes=[('pT', s)])
                P.op('dve', (lambda e: e.tensor_tensor(out=pT[:, s, c_lo:c_hi], in0=pT[:, s, c_lo:c_hi],
                                                       in1=mask4[:, c_lo:c_hi], op=ALU.mult)),
                     reads=[('pT', s)], writes=[('pT', s)])

            def pv_stage(st):
                (bi, d, Lr, qsrc, ksrc, vview, qk_keys, vkey, r, n0, ng, nn, nb2, g_i, i) = st
                s = i % 3
                bO = 3 + g_i % 2
                bD = 5 + g_i % 2
                for u in range(nb2):
                    n_ = n0 + nn + u
                    oc = slice((nn + u) * 128, (nn + u + 1) * 128)
                    lastu = (u == nb2 - 1)
                    if n_ > 0:
                        mm(ps[bO][:, oc], vview[:, r, n_ - 1, :], pT[:, s, u * 256:u * 256 + 128], True, False,
                           reads=[('pT', s), vkey], writes=[('ps', bO)], tick=False)
                        mm(ps[bD][:, oc], ones_b[:], pT[:, s, u * 256:u * 256 + 128], True, False,
                           reads=[('pT', s)], writes=[('ps', bD)], tick=False)
                    mm(ps[bO][:, oc], vview[:, r, n_, :], pT[:, s, u * 256 + 128:u * 256 + 256], n_ == 0, True,
                       reads=[('pT', s), vkey], writes=[('ps', bO)], tick=False)
                    mm(ps[bD][:, oc], ones_b[:], pT[:, s, u * 256 + 128:u * 256 + 256], n_ == 0, True,
                       reads=[('pT', s)], writes=[('ps', bD)], tick=lastu)
                if nn + nb2 == ng:
                    nview = num[:].rearrange("p (m r) -> p r m", r=d)[:, r, n0 * 128:(n0 + ng) * 128]
                    dview = den[:].rearrange("p (m r) -> p r m", r=d)[:, r, n0 * 128:(n0 + ng) * 128]
                    w_ = ng * 128
                    if bi == 0:
                        P.op('act', (lambda e: e.copy(out=nview, in_=ps[bO][:, 0:w_])), reads=[('ps', bO)], writes=['num'])
                        P.op('dve', (lambda e: e.tensor_copy(out=dview, in_=ps[bD][:, 0:w_])), reads=[('ps', bD)], writes=['den'])
                    else:
                        P.op('dve', (lambda e: e.tensor_tensor(out=nview, in0=nview, in1=ps[bO][:, 0:w_], op=ALU.add)),
                             reads=[('ps', bO), 'num'], writes=['num'])
                        P.op('dve', (lambda e: e.tensor_tensor(out=dview, in0=dview, in1=ps[bD][:, 0:w_], op=ALU.add)),
                             reads=[('ps', bD), 'den'], writes=['den'])

            def head_steps(h):
                hp = h % 2
                steps = []
                for bi, d in enumerate(DIL):
                    Lr = S // d
                    nblk = Lr // 128
                    if d == 1:
                        qsrc, ksrc = qT2[:, hp, :], kT2[:, hp, :]
                        qk_keys = [('qT', hp), ('kT', hp)]
                    else:
                        qsrc, ksrc = qdb[d][:, hp, :], kdb[d][:, hp, :]
                        qk_keys = [('qd', d, hp), ('kd', d, hp)]
                    vview = vdb[d][:, hp, :].rearrange("p (r kb c) -> p r kb c", r=d, c=128)
                    vkey = ('vd', d, hp)
                    for r in range(d):
                        for n0 in range(0, nblk, 4):
                            ng = min(4, nblk - n0)
                            g_i = grp[0]
                            grp[0] += 1
                            for nn in range(0, ng, 2):
                                nb2 = min(2, ng - nn)
                                steps.append((bi, d, Lr, qsrc, ksrc, vview, qk_keys, vkey, r, n0, ng, nn, nb2, g_i, it[0]))
                                it[0] += 1
                return steps

            def finalize(h):
                for j in range(NT):
                    s = j % 2
                    P.op('dve', (lambda e, j=j: e.reciprocal(out=rden[:], in_=den[:, j * T:(j + 1) * T])),
                         reads=['den'], writes=['rden'])
                    P.op('dve', (lambda e, j=j, s=s: e.tensor_tensor(out=ostg[:, s, :], in0=num[:, j * T:(j + 1) * T], in1=rden[:],
                                                                     op=ALU.mult)),
                         reads=['num', 'rden'], writes=[('ostg', s)])
                    P.dma('sp', mixT[12 + h][:, j * T:(j + 1) * T], ostg[:, s, :], reads=[('ostg', s)],
                          writes=[('mixT', 12 + h, j)])

            prep_loads(0)
            prep_copies(0)
            for h in range(4):
                if h + 1 < 4:
                    prep_loads(h + 1)
                steps = head_steps(h)
                nst = len(steps)
                mid = nst // 3
                for i_ in range(nst + 2):
                    if i_ == mid and h + 1 < 4:
                        prep_copies(h + 1)
                    if i_ < nst:
                        qk_stage(steps[i_])
                    if i_ >= 2:
                        pv_stage(steps[i_ - 2])
                finalize(h)
            P.barrier()

    def phase_m4(l):
        with ExitStack() as ph:
            xt2 = sb("xt4", [128, 2, KC, T], F32, ph)
            mt2 = sb("mt", [128, 2, KC, T], BF16, ph)
            wob = sb("wob", [128, 8, 16, 256], BF16, ph)

            def loads(tt):
                s_ = tt % 2
                P.dma('sp', mt2[:, s_], mixT[:, :, tt * T:(tt + 1) * T].rearrange("c p t -> p c t"), writes=[('mt', s_)])
                P.dma('sp', xt2[:, s_], xT[:, tt * T:(tt + 1) * T].rearrange("(c p) t -> p c t", p=128),
                      reads=[('xT', tt)], writes=[('xt', s_, c) for c in range(KC)])

            P.dma('sp', wob[:, 0], wout_s[l][0], reads=[('wout_s', l, 0)], writes=[('wob', 0)])
            loads(0)
            for g in range(1, 8):
                P.dma('sp', wob[:, g], wout_s[l][g], reads=[('wout_s', l, g)], writes=[('wob', g)])
            for tt in range(NT):
                s_ = tt % 2
                if tt + 1 < NT:
                    loads(tt + 1)
                for g in range(8):
                    for j in range(2):
                        o = 2 * g + j
                        bY = o % 4
                        for k in range(KC):
                            mm(ps[bY][:], wob[:, g, k, j * 128:(j + 1) * 128], mt2[:, s_, k, :], k == 0, k == KC - 1,
                               reads=[('wob', g), ('mt', s_)], writes=[('ps', bY)], tick=(k == KC - 1))
                        P.op('dve', (lambda e, o=o, bY=bY, s_=s_: e.tensor_tensor(out=xt2[:, s_, o, :], in0=ps[bY][:],
                                                                                 in1=xt2[:, s_, o, :], op=ALU.add)),
                             reads=[('ps', bY), ('xt', s_, o)], writes=[('xt', s_, o)])
                P.dma('sp', xT[:, tt * T:(tt + 1) * T].rearrange("(c p) t -> p c t", p=128), xt2[:, s_],
                      reads=[('xt', s_, c) for c in range(KC)], writes=[('xT', tt)])
            P.barrier()

    def phase_final():
        gbase = SM['FIN']
        with ExitStack() as ph:
            xt = sb("xtf", [128, KC, T], F32, ph)
            sq = sb("sqf", [128, 2, T], F32, ph)
            acc = sb("accf", [128, T], F32, ph)
            rstd = sb("rstdf", [128, T], F32, ph)
            ost = sb("ost", [128, 4, D], F32, ph)
            for tt in range(NT):
                load_xt(xt, tt)
                rmsnorm(lambda c: xt[:, c, :], KC, lambda c: smalls[:, gbase + c:gbase + c + 1], lambda c: xt[:, c, :],
                        D, lambda c: ('xt', c), lambda c: ('xt', c), sq, rstd, 6, 'rstd', acc)
                i = 0
                for b in range(4):
                    for c4 in range(4):
                        bank = i % 4
                        i += 1
                        for cc in range(4):
                            c = c4 * 4 + cc
                            o_ap = ps[bank][:, cc * 128:(cc + 1) * 128]
                            i_ap = xt[:, c, b * 128:(b + 1) * 128]
                            P.op('pe', (lambda e, o_ap=o_ap, i_ap=i_ap: e.transpose(out=o_ap, in_=i_ap, identity=ident)),
                                 reads=[('xt', c)], writes=[('ps', bank)], tick=(cc == 3))
                        copy_any(ost[:, b, c4 * 512:(c4 + 1) * 512], ps[bank][:], [('ps', bank)], [('ost', b, c4)])
                P.dma('sp', out[tt * T:(tt + 1) * T, :].rearrange("(b p) d -> p b d", p=128), ost[:],
                      reads=[('ost', b, c4) for b in range(4) for c4 in range(4)], writes=[('out', tt)])
            P.barrier()

    phase_T0()
    for l in range(L):
        P.gate('pool', 'pe')
        conv_ffn(l, 1)
        phase_ffn(l, 0)
        phase_m1(l)
        phase_m2(l)
        if l + 1 < L:
            P.gate('pool', 'pe')
            conv_mix(l + 1)
            conv_ffn(l + 1, 0)
        phase_m3(l)
        phase_m3c(l)
        phase_m4(l)
        phase_ffn(l, 1)
    phase_final()
    P.replay()
    P.close()
    return nc


def _rope_tables(S):
    t = np.arange(S, dtype=np.float32)

    def tab(dim):
        inv = (1.0 / (np.float32(THETA) ** (np.arange(0, dim, 2, dtype=np.float32) / np.float32(dim)))).astype(np.float32)
        ang = (t[:, None] * inv[None, :]).astype(np.float32)
        return np.cos(ang).astype(np.float32).T, np.sin(ang).astype(np.float32).T
    cC, sC = tab(32)
    ropeC = np.zeros((2, 128, S), np.float32)
    ropeC[0, :, :] = 1.0
    ropeC[0, 0:16] = cC
    ropeC[0, 16:32] = cC
    ropeC[1, 0:16] = -sC
    ropeC[1, 16:32] = sC
    cM, sM = tab(64)
    ropeM = np.zeros((2, 128, S), np.float32)
    for hf in range(2):
        ropeM[0, hf * 64:hf * 64 + 32] = cM
        ropeM[0, hf * 64 + 32:hf * 64 + 64] = cM
        ropeM[1, hf * 64:hf * 64 + 32] = -sM
        ropeM[1, hf * 64 + 32:hf * 64 + 64] = sM
    return ropeC, ropeM


def _win_cols():
    fq, fk, fv, fl, cq, ckv, kr, dq, dk, dv = 0, 512, 1024, 1536, 1540, 2052, 2564, 2628, 3140, 3652
    cols = []
    cols += list(range(fq, fq + 512)) + list(range(fk, fk + 512)) + list(range(cq, cq + 512)) + list(range(ckv, ckv + 512))
    krc = list(range(kr, kr + 64))
    krs = list(range(kr + 32, kr + 64)) + list(range(kr, kr + 32))
    cols += krc + krc + krs + krs
    for base in (dq, dk):
        for h in range(4):
            b = base + h * 128
            cols += list(range(b, b + 128))
            cols += list(range(b + 16, b + 32)) + list(range(b, b + 16)) + list(range(b + 32, b + 128))
    cols += list(range(fv, fv + 512)) + list(range(dv, dv + 512)) + list(range(fl, fl + 4))
    assert len(cols) == NINX
    return np.array(cols)


def _wuq_cols():
    cols = []
    for h in range(8):
        cols += list(range(h * 192, h * 192 + 128))
    for h in range(8):
        cols += list(range(h * 192 + 128, h * 192 + 192))
    for h in range(8):
        cols += list(range(h * 192 + 160, h * 192 + 192)) + list(range(h * 192 + 128, h * 192 + 160))
    return np.array(cols)


def _wukv_cols():
    cols = []
    for h in range(8):
        cols += list(range(h * 256, h * 256 + 128))
    for h in range(8):
        cols += list(range(h * 256 + 128, h * 256 + 256))
    return np.array(cols)


def prep_shared(inp, L, S):
    f = lambda a: np.ascontiguousarray(np.asarray(a, dtype=np.float32))
    SM = smalls_layout(L)
    smalls = np.zeros((128, SM['N']), np.float32)
    for nm, key in (('F1', 'ffn1_norm'), ('MX', 'mix_norm'), ('F2', 'ffn2_norm')):
        g = f(inp[key])[:L]
        smalls[:, SM[nm]:SM[nm] + L * 16] = g.reshape(L, 16, 128).transpose(2, 0, 1).reshape(128, L * 16)
    smalls[:, SM['FIN']:SM['FIN'] + 16] = f(inp['final_norm']).reshape(16, 128).T
    smalls[:, SM['QN']:SM['QN'] + 4 * L] = f(inp['mla_q_norm'])[:L].reshape(L, 4, 128).transpose(2, 0, 1).reshape(128, L * 4)
    smalls[:, SM['KVN']:SM['KVN'] + 4 * L] = f(inp['mla_kv_norm'])[:L].reshape(L, 4, 128).transpose(2, 0, 1).reshape(128, L * 4)
    smalls[0:4, SM['FB']:SM['FB'] + L] = f(inp['fox_forget_bias'])[:L].T
    consts = np.zeros((128, 384), np.float32)
    consts[:, 0:128] = np.eye(128, dtype=np.float32)
    kk = np.arange(128)[:, None]
    qq = np.arange(128)[None, :]
    consts[:, 128:256] = (qq <= kk).astype(np.float32)
    consts[:, 256:384] = (qq >= kk).astype(np.float32)
    ropeC, ropeM = _rope_tables(S)
    sh = {
        "ffn1_w_gate": f(inp['ffn1_w_gate'])[:L], "ffn1_w_up": f(inp['ffn1_w_up'])[:L], "ffn1_w_down": f(inp['ffn1_w_down'])[:L],
        "ffn2_w_gate": f(inp['ffn2_w_gate'])[:L], "ffn2_w_up": f(inp['ffn2_w_up'])[:L], "ffn2_w_down": f(inp['ffn2_w_down'])[:L],
        "w_in_ext": np.ascontiguousarray(f(inp['w_in'])[:L][:, :, _win_cols()]),
        "w_uq_ext": np.ascontiguousarray(f(inp['mla_w_uq'])[:L][:, :, _wuq_cols()]),
        "w_ukv_ext": np.ascontiguousarray(f(inp['mla_w_ukv'])[:L][:, :, _wukv_cols()]),
        "w_out": f(inp['w_out'])[:L],
        "smalls": smalls, "consts": consts, "ropeC": ropeC, "ropeM": ropeM,
    }
    return sh


_NC_CACHE = {}


def kernel(**inputs):
    x = np.asarray(inputs['x'], dtype=np.float32)
    B, S, _ = x.shape
    L = int(np.asarray(inputs['w_out']).shape[0])
    key = (S, L)
    if key not in _NC_CACHE:
        _NC_CACHE[key] = build(S, L)
    nc = _NC_CACHE[key]
    sh = prep_shared(inputs, L, S)
    in_maps = []
    for b in range(B):
        m = dict(sh)
        m["x"] = np.ascontiguousarray(x[b])
        in_maps.append(m)
    res = run_bass_kernel_spmd(nc, in_maps, core_ids=list(range(B)))
    return np.stack([np.asarray(r["out"], dtype=np.float32) for r in res.results], axis=0)
```
